# Optimizing a Trainium2 kernel written in Bass

```python
import math
import jax, jax.numpy as jnp
from jax import lax
import numpy as np

D_MODEL = 1024
BATCH = 8
SEQ = 4096
DEPTH = 1

MIX_WIDTH = D_MODEL
ATT_WIDTH = MIX_WIDTH // 2
POOL_WIDTH = MIX_WIDTH - ATT_WIDTH
HEAD_DIM = 64
N_Q_HEADS = ATT_WIDTH // HEAD_DIM
N_KV_HEADS = 2
Q_PER_KV = N_Q_HEADS // N_KV_HEADS
KV_WIDTH = N_KV_HEADS * HEAD_DIM
Q_BLOCK = 128
GRID_W = 64
AXIS_ROPE_DIM = HEAD_DIM // 2
ROPE_THETA = 10000.0
POOL_WINDOWS = (2, 4, 8, 16)
N_POOL_GROUPS = len(POOL_WINDOWS)
POOL_GROUP_DIM = POOL_WIDTH // N_POOL_GROUPS
IN_PROJ_WIDTH = ATT_WIDTH + 2 * KV_WIDTH + POOL_WIDTH
D_FF = 2816
CONV_W = 3
N_MOD = 6
EPS = 1e-6

kernel_name = "hybrid_gqa_pool_convffn_adaln_block"


def rmsnorm(x, g):
    xf = x.astype(jnp.float32)
    y = xf * lax.rsqrt(jnp.mean(xf * xf, axis=-1, keepdims=True) + EPS)
    return (y * g.astype(jnp.float32)).astype(x.dtype)


def axial_rope_tables(T):
    rows = T // GRID_W
    row = jnp.repeat(jnp.arange(rows), GRID_W).astype(jnp.float32)
    col = jnp.tile(jnp.arange(GRID_W), rows).astype(jnp.float32)
    inv = 1.0 / (ROPE_THETA ** (jnp.arange(0, AXIS_ROPE_DIM, 2, dtype=jnp.float32) / AXIS_ROPE_DIM))
    ang_r = row[:, None] * inv[None, :]
    ang_c = col[:, None] * inv[None, :]
    return jnp.cos(ang_r), jnp.sin(ang_r), jnp.cos(ang_c), jnp.sin(ang_c)


def rotate_chunk(u, cos, sin):
    half = AXIS_ROPE_DIM // 2
    u1, u2 = u[..., :half], u[..., half:]
    cos = cos[None, :, None, :]
    sin = sin[None, :, None, :]
    return jnp.concatenate([u1 * cos - u2 * sin, u2 * cos + u1 * sin], axis=-1)


def apply_axial_rope(u, tables):
    cr, sr, cc, sc = tables
    uf = u.astype(jnp.float32)
    out = jnp.concatenate([rotate_chunk(uf[..., :AXIS_ROPE_DIM], cr, sr),
                           rotate_chunk(uf[..., AXIS_ROPE_DIM:], cc, sc)], axis=-1)
    return out.astype(u.dtype)


def blocked_gqa(q, k, v):
    B, T = q.shape[0], q.shape[1]
    nb = T // Q_BLOCK
    q = q * jnp.asarray(1.0 / math.sqrt(HEAD_DIM), q.dtype)
    qb = q.reshape(B, nb, Q_BLOCK, N_KV_HEADS, Q_PER_KV, HEAD_DIM).transpose(1, 0, 2, 3, 4, 5)

    def one_block(qblk):
        s = jnp.einsum('bqkgd,bskd->bkgqs', qblk, k).astype(jnp.float32)
        p = jax.nn.softmax(s, axis=-1).astype(v.dtype)
        return jnp.einsum('bkgqs,bskd->bqkgd', p, v)

    o = lax.map(one_block, qb)
    return o.transpose(1, 0, 2, 3, 4, 5).reshape(B, T, N_Q_HEADS * HEAD_DIM)


def centred_mean_minus_self(u, w):
    B, T, C = u.shape
    uf = u.astype(jnp.float32)
    S = jnp.concatenate([jnp.zeros((B, 1, C), jnp.float32), jnp.cumsum(uf, axis=1)], axis=1)
    t = jnp.arange(T)
    lo = jnp.maximum(t - w // 2, 0)
    hi = jnp.minimum(t + w // 2 - 1, T - 1)
    win = S[:, hi + 1] - S[:, lo]
    cnt = (hi - lo + 1).astype(jnp.float32)[None, :, None]
    return (win / cnt - uf).astype(u.dtype)


def pool_mixer(u, w_pool, pool_scale):
    B, T, _ = u.shape
    ug = u.reshape(B, T, N_POOL_GROUPS, POOL_GROUP_DIM)
    pooled = jnp.stack([centred_mean_minus_self(ug[:, :, g], POOL_WINDOWS[g])
                        for g in range(N_POOL_GROUPS)], axis=2)
    mixed = jnp.einsum('btgc,gcd->btgd', pooled, w_pool)
    return mixed.reshape(B, T, POOL_WIDTH) * pool_scale


def conv_gated_ffn(h, w_up, conv_w, conv_b, w_down):
    u = h @ w_up
    up = jnp.pad(u, ((0, 0), (1, 1), (0, 0)))
    u = up[:, :-2] * conv_w[0] + up[:, 1:-1] * conv_w[1] + up[:, 2:] * conv_w[2] + conv_b
    gate, val = u[..., :D_FF], u[..., D_FF:]
    return (jax.nn.silu(gate) * val) @ w_down


def setup_inputs(seed: int = 0) -> dict:
    key = jax.random.key(seed)
    ks = jax.random.split(key, 16)
    f32 = jnp.float32
    nrm = lambda k, shape, s: jax.random.normal(k, shape, f32) * s
    return {
        "x": nrm(ks[0], (BATCH, SEQ, D_MODEL), 1.0),
        "c": nrm(ks[1], (BATCH, D_MODEL), 1.0),
        "w_ada": nrm(ks[2], (DEPTH, D_MODEL, N_MOD * D_MODEL), 0.5 * D_MODEL ** -0.5),
        "b_ada": nrm(ks[3], (DEPTH, N_MOD * D_MODEL), 0.01),
        "norm1_g": 1.0 + nrm(ks[4], (DEPTH, D_MODEL), 0.02),
        "w_in": nrm(ks[5], (DEPTH, D_MODEL, IN_PROJ_WIDTH), D_MODEL ** -0.5),
        "q_norm_g": 1.0 + nrm(ks[6], (DEPTH, HEAD_DIM), 0.02),
        "k_norm_g": 1.0 + nrm(ks[7], (DEPTH, HEAD_DIM), 0.02),
        "w_pool": nrm(ks[8], (DEPTH, N_POOL_GROUPS, POOL_GROUP_DIM, POOL_GROUP_DIM), POOL_GROUP_DIM ** -0.5),
        "pool_scale": 1.0 + nrm(ks[9], (DEPTH, POOL_WIDTH), 0.1),
        "w_out": nrm(ks[10], (DEPTH, MIX_WIDTH, D_MODEL), MIX_WIDTH ** -0.5),
        "norm2_g": 1.0 + nrm(ks[11], (DEPTH, D_MODEL), 0.02),
        "w_up": nrm(ks[12], (DEPTH, D_MODEL, 2 * D_FF), D_MODEL ** -0.5),
        "conv_w": nrm(ks[13], (DEPTH, CONV_W, 2 * D_FF), CONV_W ** -0.5),
        "conv_b": nrm(ks[14], (DEPTH, 2 * D_FF), 0.01),
        "w_down": nrm(ks[15], (DEPTH, D_FF, D_MODEL), D_FF ** -0.5),
    }


def reference(x, c, w_ada, b_ada, norm1_g, w_in, q_norm_g, k_norm_g, w_pool, pool_scale,
              w_out, norm2_g, w_up, conv_w, conv_b, w_down):
    B, T, D = x.shape
    tables = axial_rope_tables(T)
    c_act = jax.nn.silu(c)
    for l in range(DEPTH):
        mod = c_act @ w_ada[l] + b_ada[l]
        sh1, sc1, g1, sh2, sc2, g2 = [m[:, None, :] for m in jnp.split(mod, N_MOD, axis=-1)]

        h = rmsnorm(x, norm1_g[l]) * (1.0 + sc1) + sh1
        proj = h @ w_in[l]
        o0 = ATT_WIDTH
        o1 = o0 + KV_WIDTH
        o2 = o1 + KV_WIDTH
        q = proj[..., :o0].reshape(B, T, N_Q_HEADS, HEAD_DIM)
        k = proj[..., o0:o1].reshape(B, T, N_KV_HEADS, HEAD_DIM)
        v = proj[..., o1:o2].reshape(B, T, N_KV_HEADS, HEAD_DIM)
        u_pool = proj[..., o2:]

        q = apply_axial_rope(rmsnorm(q, q_norm_g[l]), tables)
        k = apply_axial_rope(rmsnorm(k, k_norm_g[l]), tables)
        att = blocked_gqa(q, k, v)
        pool = pool_mixer(u_pool, w_pool[l], pool_scale[l])

        mix = jnp.concatenate([att, pool], axis=-1) @ w_out[l]
        x = x + g1 * mix

        h2 = rmsnorm(x, norm2_g[l]) * (1.0 + sc2) + sh2
        x = x + g2 * conv_gated_ffn(h2, w_up[l], conv_w[l], conv_b[l], w_down[l])
    return x
```

```python
import math
import contextlib
import numpy as np
import concourse.bass as bass
import concourse.mybir as mybir
from concourse.bass_utils import run_bass_kernel_spmd

F32 = mybir.dt.float32
BF16 = mybir.dt.bfloat16
AF = mybir.ActivationFunctionType
ALU = mybir.AluOpType
AX = mybir.AxisListType

T = 4096
D = 1024
NTT = 32
DFF = 2816
NCH = 22
EPS = 1e-6
ARENA_BYTES = 204800

PE, ACT, DVE, POOL, SP = "tensor", "scalar", "vector", "gpsimd", "sync"
ENGS = [PE, ACT, DVE, POOL, SP]


class Sem:
    def __init__(self, h, idx):
        self.h = h
        self.idx = idx
        self.count = 0


class Buf:
    def __init__(self, name):
        self.name = name
        self.w = None
        self.r = {}


class Builder:
    def __init__(self, nc, stack):
        self.nc = nc
        self.stack = stack
        self.nsem = 0
        self.q = {e: [] for e in ENGS}
        self.esem = {e: self.new_sem("e_" + e) for e in ENGS}
        self.waited = {e: {} for e in ENGS}
        self.dma_sems = []

    def new_sem(self, name):
        h = self.stack.enter_context(self.nc.semaphore(name))
        s = Sem(h, self.nsem)
        self.nsem += 1
        return s

    def new_dma_sem(self, name):
        s = self.new_sem(name)
        self.dma_sems.append(s)
        return s

    def _collect(self, eng, reads, writes):
        evs = []
        for b in reads:
            if b.w is not None:
                evs.append(b.w)
        for b in writes:
            if b.w is not None and b.w[2] != eng:
                evs.append(b.w)
            for ev in b.r.values():
                if ev[2] != eng:
                    evs.append(ev)
        out = []
        for (s, v, pe) in evs:
            if eng == PE and pe == PE:
                continue
            if self.waited[eng].get(s.idx, 0) < v:
                self.waited[eng][s.idx] = v
                out.append((s.h, v))
        return out

    def wait_only(self, eng, reads=(), writes=()):
        w = self._collect(eng, reads, writes)
        if w:
            self.q[eng].append((w, None, None, 0))

    def raw(self, eng, fn):
        self.q[eng].append(([], fn, None, 0))

    def op(self, eng, fn, reads=(), writes=(), dma=None):
        w = self._collect(eng, reads, writes)
        if dma is None:
            s = self.esem[eng]
            s.count += 1
            inc = 1
        else:
            s = dma
            s.count += 16
            inc = 16
        ev = (s, s.count, eng if dma is None else "dma")
        self.q[eng].append((w, fn, s.h, inc))
        for b in reads:
            b.r[s.idx] = ev
        for b in writes:
            b.w = ev
            b.r = {}
        return ev

    def barrier(self):
        for e in ENGS:
            w = []
            for e2 in ENGS:
                s = self.esem[e2]
                if s.count > 0 and self.waited[e].get(s.idx, 0) < s.count:
                    self.waited[e][s.idx] = s.count
                    w.append((s.h, s.count))
            for s in self.dma_sems:
                if s.count > 0 and self.waited[e].get(s.idx, 0) < s.count:
                    self.waited[e][s.idx] = s.count
                    w.append((s.h, s.count))
            if w:
                self.q[e].append((w, None, None, 0))

    def emit(self):
        with self.nc.Block() as block:
            def mk(name):
                def f(e):
                    for waits, fn, sem, inc in self.q[name]:
                        for (h, v) in waits:
                            e.wait_ge(h, v)
                        if fn is not None:
                            ins = fn(e)
                            if sem is not None:
                                ins.then_inc(sem, inc)
                return f
            block.tensor(mk(PE))
            block.scalar(mk(ACT))
            block.vector(mk(DVE))
            block.gpsimd(mk(POOL))
            block.sync(mk(SP))


class Arena:
    def __init__(self, t):
        self.t = t
        self.off = 0

    def alloc(self, nelem, dt):
        esz = 4 if dt == F32 else 2
        sz = (nelem * esz + 63) // 64 * 64
        o = self.off
        self.off += sz
        assert self.off <= ARENA_BYTES, ("arena overflow", self.off)
        ap = self.t[:, o // 2:(o + sz) // 2]
        if dt == F32:
            ap = ap.bitcast(F32)
        return ap[:, 0:nelem]


def build_program(dbg=False):
    nc = bass.Bass("TRN2", target_bir_lowering=False)

    def din(name, shape, dt=F32):
        return nc.dram_tensor(name, shape, dt, kind="ExternalInput").ap()

    x = din("x", [T, D])
    c_l = din("c_l", [128, 8])
    w_ada = din("w_ada_l", [128, 12 * 4096])
    b_ada = din("b_ada", [1, 6144])
    n1g = din("n1g", [1, D])
    n2g = din("n2g", [1, D])
    w_in = din("w_in_l", [128, 8 * 1280])
    gvec = din("gvec", [1, 640])
    w_pool = din("w_pool_l", [128, 512])
    pscale_d = din("pscale_l", [128, 4])
    w_out = din("w_out_l", [128, 8 * 1024])
    w_up = din("w_up_l", [128, NCH * 2048])
    convp_d = din("convp", [128, 176])
    w_down = din("w_down_l", [128, NCH * 1024])
    ropeC_d = din("ropeC", [128, 2048])
    ropeS_d = din("ropeS", [128, 2048])
    band_d = din("band", [128, 2560])
    ident_d = din("ident", [128, 128])
    out = nc.dram_tensor("out", [T, D], F32, kind="ExternalOutput").ap()
    w_up_bf = nc.dram_tensor("w_up_bf", [128, NCH * 2048], BF16, kind="Internal").ap()
    w_out_bf = nc.dram_tensor("w_out_bf", [128, 8192], BF16, kind="Internal").ap()
    w_down_bf = nc.dram_tensor("w_down_bf", [128, NCH * 1024], BF16, kind="Internal").ap()
    x1s = nc.dram_tensor("x1s", [T + 2, D], F32, kind="Internal").ap()
    if dbg:
        d_qt = nc.dram_tensor("d_qt", [128, 4 * T], BF16, kind="ExternalOutput").ap()
        d_kt = nc.dram_tensor("d_kt", [128, T], BF16, kind="ExternalOutput").ap()
        d_v = nc.dram_tensor("d_v", [128, NTT * 192], BF16, kind="ExternalOutput").ap()
        d_pt = nc.dram_tensor("d_pt", [128, 4 * T], BF16, kind="ExternalOutput").ap()
        d_mod = nc.dram_tensor("d_mod", [128, 6144], F32, kind="ExternalOutput").ap()
        d_x1 = nc.dram_tensor("d_x1", [T + 2, D], F32, kind="ExternalOutput").ap()

    with contextlib.ExitStack() as stack:
        arena_t = stack.enter_context(nc.sbuf_tensor("arena", [128, ARENA_BYTES // 2], BF16))
        banks = [stack.enter_context(nc.psum_tensor("bank%d" % i, [128, 512], F32)) for i in range(8)]
        bankb = [Buf("bank%d" % i) for i in range(8)]
        B = Builder(nc, stack)
        A = Arena(arena_t)

        sh1 = A.alloc(1024, F32)
        a1 = A.alloc(1024, F32)
        sh2 = A.alloc(1024, F32)
        a2 = A.alloc(1024, F32)
        gv = A.alloc(640, F32)
        negM = A.alloc(16, F32)
        ident_bf = A.alloc(128, BF16)
        band_bf = A.alloc(2560, BF16)
        wpool_bf = A.alloc(512, BF16)
        pscale = A.alloc(16, F32)
        convp = A.alloc(176, F32)
        ones_f = A.alloc(128, F32)
        w_in_bf = A.alloc(8 * 1280, BF16)
        base = A.off
        QT = A.alloc(4 * T, BF16)
        KT = A.alloc(T, BF16)
        VA = A.alloc(NTT * 192, BF16)
        PT = A.alloc(4 * T, BF16)
        ab_end = A.off
        QT3 = QT.rearrange("p (c t) -> p c t", c=4)
        VA3 = VA.rearrange("p (k c) -> p k c", c=192)
        PT3 = PT.rearrange("p (g t) -> p g t", g=4)

        A.off = base
        badabc = A.alloc(6144, F32)
        modbc = A.alloc(6144, F32)
        n1gbc = A.alloc(1024, F32)
        n2gbc = A.alloc(1024, F32)
        csb = A.alloc(16, F32)
        cact = A.alloc(16, F32)
        cbc = A.alloc(1024, F32)
        mq = A.alloc(16, F32)
        wada_s = [A.alloc(4096, F32) for _ in range(2)]
        stin = [A.alloc(2048, F32) for _ in range(2)]
        stout = [A.alloc(2048, BF16) for _ in range(2)]
        zrow = A.alloc(1024, F32)

        b_consts = Buf("consts")
        b_small = Buf("small")
        b_cbc = Buf("cbc")
        b_mod = Buf("mod")
        b_wada = [Buf("wada%d" % i) for i in range(2)]
        b_stin = [Buf("stin%d" % i) for i in range(2)]
        b_stout = [Buf("stout%d" % i) for i in range(2)]
        s_wada = [B.new_dma_sem("s_wada%d" % i) for i in range(2)]
        s_stin = [B.new_dma_sem("s_stin%d" % i) for i in range(2)]
        s_stout = [B.new_dma_sem("s_stout%d" % i) for i in range(2)]
        s_misc = B.new_dma_sem("s_misc")
        s_cast = B.new_dma_sem("s_cast")
        s_cast2 = B.new_dma_sem("s_cast2")

        small_loads = [
            (csb[:, 0:8], c_l),
            (badabc, b_ada.partition_broadcast(128)),
            (n1gbc, n1g.partition_broadcast(128)),
            (n2gbc, n2g.partition_broadcast(128)),
            (gv, gvec.partition_broadcast(128)),
            (pscale[:, 0:4], pscale_d),
            (convp, convp_d),
        ]
        for i, (o_, i_) in enumerate(small_loads):
            last = i == len(small_loads) - 1
            if last:
                B.op(SP, (lambda o_=o_, i_=i_: lambda e: e.dma_start(out=o_, in_=i_))(),
                     writes=[b_consts], dma=s_misc)
            else:
                s_misc.count += 16
                B.q[SP].append(([], (lambda o_=o_, i_=i_: lambda e: e.dma_start(out=o_, in_=i_))(), s_misc.h, 16))
        B.op(POOL, lambda e: e.dma_start(out=w_in_bf.rearrange("p (a b) -> p a b", b=1280),
                                         in_=w_in.rearrange("p (a b) -> p a b", b=1280)),
             writes=[Buf("win")], dma=s_cast)
        b_c2 = Buf("c2")
        s_cast2.count += 32
        B.q[POOL].append(([], lambda e: e.dma_start(out=ident_bf, in_=ident_d), s_cast2.h, 16))
        B.q[POOL].append(([], lambda e: e.dma_start(out=band_bf.rearrange("p (a b) -> p a b", b=1280), in_=band_d.rearrange("p (a b) -> p a b", b=1280)), s_cast2.h, 16))
        B.op(POOL, lambda e: e.dma_start(out=wpool_bf, in_=w_pool), writes=[b_c2], dma=s_cast2)
        s_cast3 = B.new_dma_sem("s_cast3")
        B.op(POOL, lambda e: e.dma_start(out=w_up_bf.rearrange("p (a b) -> p a b", b=2048),
                                         in_=w_up.rearrange("p (a b) -> p a b", b=2048)),
             writes=[Buf("wupbf")], dma=s_cast3)

        B.op(DVE, lambda e: e.memset(ones_f, 1.0), writes=[b_small])
        B.op(DVE, lambda e: e.memset(zrow, 0.0), writes=[b_small])
        B.op(ACT, lambda e: e.activation(out=cact[:, 0:8], in_=csb[:, 0:8], func=AF.Silu),
             reads=[b_consts], writes=[b_small])
        B.op(DVE, lambda e: e.tensor_copy(out=cbc.rearrange("p (a b) -> p a b", a=8),
                                          in_=cact[:, 0:8].unsqueeze(2).to_broadcast([128, 8, 128])),
             reads=[b_small], writes=[b_cbc])
        s_z = B.new_dma_sem("s_z")
        s_z2 = B.new_dma_sem("s_z2")
        B.op(SP, lambda e: e.dma_start(out=x1s[0:1, :], in_=zrow[0:1, :]), reads=[b_small], writes=[Buf("z0")], dma=s_z)
        B.op(SP, lambda e: e.dma_start(out=x1s[T + 1:T + 2, :], in_=zrow[0:1, :]), reads=[b_small], writes=[Buf("z1")], dma=s_z2)

        for nt in range(12):
            sl = nt % 2
            B.op(SP, (lambda nt=nt, sl=sl: lambda e: e.dma_start(out=wada_s[sl], in_=w_ada[:, nt * 4096:(nt + 1) * 4096]))(),
                 writes=[b_wada[sl]], dma=s_wada[sl])
            bk = nt % 2
            B.wait_only(PE, reads=[b_wada[sl], b_cbc], writes=[bankb[bk]])
            for kc in range(8):
                fn = (lambda kc=kc, sl=sl, bk=bk: lambda e: e.matmul(
                    banks[bk][:, :], lhsT=cbc[:, kc * 128:(kc + 1) * 128],
                    rhs=wada_s[sl][:, kc * 512:(kc + 1) * 512], start=(kc == 0), stop=(kc == 7)))()
                if kc < 7:
                    B.raw(PE, fn)
                else:
                    B.op(PE, fn, reads=[b_wada[sl], b_cbc], writes=[bankb[bk]])
            B.op(DVE, (lambda nt=nt, bk=bk: lambda e: e.tensor_tensor(
                out=modbc[:, nt * 512:(nt + 1) * 512], in0=banks[bk][:, :],
                in1=badabc[:, nt * 512:(nt + 1) * 512], op=ALU.add))(),
                reads=[bankb[bk], b_consts], writes=[b_mod])
        b_der = Buf("derived")
        B.op(DVE, lambda e: e.scalar_tensor_tensor(out=a1, in0=modbc[:, 1024:2048], scalar=1.0, in1=n1gbc,
                                                   op0=ALU.add, op1=ALU.mult), reads=[b_mod, b_consts], writes=[b_der])
        B.op(DVE, lambda e: e.scalar_tensor_tensor(out=a2, in0=modbc[:, 4096:5120], scalar=1.0, in1=n2gbc,
                                                   op0=ALU.add, op1=ALU.mult), reads=[b_mod, b_consts], writes=[b_der])
        B.op(DVE, lambda e: e.tensor_copy(out=sh1, in_=modbc[:, 0:1024]), reads=[b_mod], writes=[b_der])
        B.op(DVE, lambda e: e.tensor_copy(out=sh2, in_=modbc[:, 3072:4096]), reads=[b_mod], writes=[b_der])
        b_m = Buf("m")
        B.op(DVE, lambda e: e.tensor_reduce(out=mq[:, 0:1], in_=gv[:, 0:64], axis=AX.X, op=ALU.max,
                                            apply_absolute_value=True), reads=[b_consts], writes=[b_m])
        B.op(DVE, lambda e: e.tensor_reduce(out=mq[:, 1:2], in_=gv[:, 512:576], axis=AX.X, op=ALU.max,
                                            apply_absolute_value=True), reads=[b_consts], writes=[b_m])
        B.op(DVE, lambda e: e.tensor_tensor(out=mq[:, 2:3], in0=mq[:, 0:1], in1=mq[:, 1:2], op=ALU.mult),
             reads=[b_m], writes=[b_m])
        B.op(DVE, lambda e: e.tensor_scalar(out=negM[:, 0:1], in0=mq[:, 2:3], scalar1=-8.0, scalar2=None,
                                            op0=ALU.mult), reads=[b_m], writes=[b_m])
        B.op(DVE, lambda e: e.tensor_scalar(out=gv[:, 0:512], in0=gv[:, 0:512], scalar1=0.125, scalar2=None,
                                            op0=ALU.mult), reads=[b_m], writes=[b_consts])
        pieces = [(w_out, w_out_bf, i, 2048) for i in range(4)] + [(w_down, w_down_bf, i, 5120) for i in range(11)]
        for k, (src, dst, i, goff) in enumerate(pieces):
            sl = k % 2
            B.op(SP, (lambda src=src, i=i, sl=sl: lambda e: e.dma_start(out=stin[sl], in_=src[:, i * 2048:(i + 1) * 2048]))(),
                 writes=[b_stin[sl]], dma=s_stin[sl])
            B.op(DVE, (lambda sl=sl, goff=goff: lambda e: e.tensor_tensor(
                out=stout[sl].rearrange("p (a b) -> p a b", a=2),
                in0=stin[sl].rearrange("p (a b) -> p a b", a=2),
                in1=modbc[:, goff:goff + 1024].unsqueeze(1).to_broadcast([128, 2, 1024]), op=ALU.mult))(),
                reads=[b_stin[sl], b_mod], writes=[b_stout[sl]])
            B.op(POOL, (lambda dst=dst, i=i, sl=sl: lambda e: e.dma_start(out=dst[:, i * 2048:(i + 1) * 2048], in_=stout[sl]))(),
                 reads=[b_stout[sl]], dma=s_stout[sl])
        if dbg:
            s_d0 = B.new_dma_sem("s_d0")
            B.op(SP, lambda e: e.dma_start(out=d_mod, in_=modbc), reads=[b_mod], dma=s_d0)
        B.barrier()

        A.off = ab_end
        ropeC = A.alloc(2048, F32)
        ropeS = A.alloc(2048, F32)
        xs = [A.alloc(1024, F32) for _ in range(2)]
        junk = A.alloc(1024, BF16)
        st = [A.alloc(16, F32) for _ in range(2)]
        htmp = A.alloc(1024, F32)
        hb = A.alloc(1024, BF16)
        hT = A.alloc(1024, BF16)
        sq = A.alloc(640, F32)
        s10 = A.alloc(32, F32)
        qn = A.alloc(640, F32)
        qg = A.alloc(640, F32)
        Aa = A.alloc(640, F32)
        Bt = A.alloc(640, F32)
        qrope = A.alloc(512, BF16)
        krope = A.alloc(128, BF16)
        upool = [A.alloc(512, BF16) for _ in range(4)]
        pld = A.alloc(512, BF16)

        b_rope = Buf("rope")
        s_rope = B.new_dma_sem("s_rope")
        s_rope.count += 16
        B.q[SP].append(([], lambda e: e.dma_start(out=ropeC, in_=ropeC_d), s_rope.h, 16))
        B.op(SP, lambda e: e.dma_start(out=ropeS, in_=ropeS_d), writes=[b_rope], dma=s_rope)
        B.op(POOL, lambda e: e.memset(VA3[:, :, 64:128], 0.0), writes=[Buf("va0")])
        B.op(POOL, lambda e: e.memset(VA3[:, :, 64:65], 1.0), writes=[Buf("va1")])

        b_xs = [Buf("xs%d" % i) for i in range(2)]
        s_xs = [B.new_dma_sem("s_xs%d" % i) for i in range(2)]
        b_st = [Buf("st%d" % i) for i in range(2)]
        b_junk, b_htmp, b_hb, b_hT, b_sq, b_s10 = Buf("junk"), Buf("htmp"), Buf("hb"), Buf("hT"), Buf("sq"), Buf("s10")
        b_qn, b_qg, b_Aa, b_Bt, b_qrope, b_krope = Buf("qn"), Buf("qg"), Buf("Aa"), Buf("Bt"), Buf("qrope"), Buf("krope")
        b_up = [Buf("up%d" % i) for i in range(4)]
        b_pld = Buf("pld")
        b_QT, b_KT, b_VA, b_PT = Buf("QT"), Buf("KT"), Buf("VA"), Buf("PT")
        tpA = banks[0][:, :].bitcast(BF16)
        tq = banks[4][:, :].bitcast(BF16)
        pj = [banks[1], banks[2], banks[3]]
        pp, po = banks[5], banks[6]

        def v3(ap, a):
            return ap.rearrange("p (a b) -> p a b", a=a)

        def pool_tile(t):
            srcs = []
            if t > 0:
                srcs.append((t - 1, 0))
            srcs.append((t, 3 if t == 0 else (4 if t == NTT - 1 else 1)))
            if t < NTT - 1:
                srcs.append((t + 1, 2))
            rd = [b_up[s % 4] for (s, _) in srcs]
            B.wait_only(PE, reads=rd, writes=[bankb[5]])
            for g in range(4):
                for si, (s, kind) in enumerate(srcs):
                    fn = (lambda g=g, s=s, kind=kind, si=si: lambda e: e.matmul(
                        pp[:, g * 128:(g + 1) * 128], lhsT=upool[s % 4][:, g * 128:(g + 1) * 128],
                        rhs=band_bf[:, (g * 5 + kind) * 128:(g * 5 + kind + 1) * 128],
                        start=(si == 0), stop=(si == len(srcs) - 1)))()
                    if g == 3 and si == len(srcs) - 1:
                        B.op(PE, fn, reads=rd, writes=[bankb[5]])
                    else:
                        B.raw(PE, fn)
            B.op(ACT, lambda e: e.activation(out=pld, in_=pp[:, :], func=AF.Identity), reads=[bankb[5]], writes=[b_pld])
            B.wait_only(PE, reads=[b_pld], writes=[bankb[6]])
            for g in range(4):
                fn = (lambda g=g: lambda e: e.matmul(po[:, g * 128:(g + 1) * 128], lhsT=wpool_bf[:, g * 128:(g + 1) * 128],
                                                     rhs=pld[:, g * 128:(g + 1) * 128], start=True, stop=True))()
                if g == 3:
                    B.op(PE, fn, reads=[b_pld], writes=[bankb[6]])
                else:
                    B.raw(PE, fn)
            for g in range(4):
                B.op(ACT, (lambda g=g, t=t: lambda e: e.activation(
                    out=PT3[:, g, t * 128:(t + 1) * 128], in_=po[:, g * 128:(g + 1) * 128], func=AF.Identity,
                    scale=pscale[:, g:g + 1]))(), reads=[bankb[6]], writes=[b_PT])

        for tt in range(NTT):
            sl = tt % 2
            B.op(SP, (lambda tt=tt, sl=sl: lambda e: e.dma_start(out=xs[sl], in_=x[tt * 128:(tt + 1) * 128, :]))(),
                 writes=[b_xs[sl]], dma=s_xs[sl])
            B.op(ACT, (lambda sl=sl: lambda e: e.activation(out=junk, in_=xs[sl], func=AF.Square, accum_out=st[sl][:, 0:1]))(),
                 reads=[b_xs[sl]], writes=[b_junk, b_st[sl]])
            B.op(ACT, (lambda sl=sl: lambda e: e.activation(out=st[sl][:, 1:2], in_=st[sl][:, 0:1], func=AF.Sqrt,
                                                             scale=1.0 / D, bias=EPS))(), reads=[b_st[sl]], writes=[b_st[sl]])
            B.op(DVE, (lambda sl=sl: lambda e: e.reciprocal(out=st[sl][:, 2:3], in_=st[sl][:, 1:2]))(),
                 reads=[b_st[sl]], writes=[b_st[sl]])
            B.op(DVE, (lambda sl=sl: lambda e: e.scalar_tensor_tensor(out=htmp, in0=xs[sl], scalar=st[sl][:, 2:3], in1=a1,
                                                                       op0=ALU.mult, op1=ALU.mult))(),
                 reads=[b_xs[sl], b_st[sl]], writes=[b_htmp])
            B.op(POOL, lambda e: e.tensor_tensor(out=hb, in0=htmp, in1=sh1, op=ALU.add), reads=[b_htmp], writes=[b_hb])
            B.wait_only(PE, reads=[b_hb], writes=[bankb[0]])
            for kc in range(8):
                fn = (lambda kc=kc: lambda e: e.transpose(tpA[:, kc * 128:(kc + 1) * 128], hb[:, kc * 128:(kc + 1) * 128], ident_bf))()
                if kc < 7:
                    B.raw(PE, fn)
                else:
                    B.op(PE, fn, reads=[b_hb], writes=[bankb[0]])
            B.op(ACT, lambda e: e.activation(out=hT, in_=tpA, func=AF.Identity), reads=[bankb[0]], writes=[b_hT])
            B.wait_only(PE, reads=[b_hT], writes=[bankb[1], bankb[2], bankb[3]])
            for kc in range(8):
                for ni, (n0, nn) in enumerate([(0, 512), (512, 512), (1024, 256)]):
                    fn = (lambda kc=kc, ni=ni, n0=n0, nn=nn: lambda e: e.matmul(
                        pj[ni][:, 0:nn], lhsT=hT[:, kc * 128:(kc + 1) * 128],
                        rhs=w_in_bf[:, kc * 1280 + n0:kc * 1280 + n0 + nn], start=(kc == 0), stop=(kc == 7)))()
                    if kc == 7 and ni == 2:
                        B.op(PE, fn, reads=[b_hT], writes=[bankb[1], bankb[2], bankb[3]])
                    else:
                        B.raw(PE, fn)
            B.op(ACT, lambda e: e.activation(out=sq[:, 0:512], in_=pj[0][:, :], func=AF.Square), reads=[bankb[1]], writes=[b_sq])
            B.op(ACT, lambda e: e.activation(out=sq[:, 512:640], in_=pj[1][:, 0:128], func=AF.Square), reads=[bankb[2]], writes=[b_sq])
            B.op(DVE, lambda e: e.tensor_reduce(out=s10[:, 0:10], in_=v3(sq, 10), axis=AX.X, op=ALU.add),
                 reads=[b_sq], writes=[b_s10])
            B.op(ACT, lambda e: e.activation(out=s10[:, 10:20], in_=s10[:, 0:10], func=AF.Sqrt, scale=1.0 / 64, bias=EPS),
                 reads=[b_s10], writes=[b_s10])
            B.op(DVE, lambda e: e.reciprocal(out=s10[:, 20:30], in_=s10[:, 10:20]), reads=[b_s10], writes=[b_s10])
            B.op(DVE, lambda e: e.tensor_tensor(out=v3(qn[:, 0:512], 8), in0=v3(pj[0][:, :], 8),
                                                in1=s10[:, 20:28].unsqueeze(2).to_broadcast([128, 8, 64]), op=ALU.mult),
                 reads=[bankb[1], b_s10], writes=[b_qn])
            B.op(DVE, lambda e: e.tensor_tensor(out=v3(qn[:, 512:640], 2), in0=v3(pj[1][:, 0:128], 2),
                                                in1=s10[:, 28:30].unsqueeze(2).to_broadcast([128, 2, 64]), op=ALU.mult),
                 reads=[bankb[2], b_s10], writes=[b_qn])
            B.op(POOL, lambda e: e.tensor_tensor(out=qg, in0=qn, in1=gv, op=ALU.mult), reads=[b_qn], writes=[b_qg])
            B.op(POOL, (lambda tt=tt: lambda e: e.tensor_tensor(
                out=v3(Aa, 10), in0=v3(qg, 10),
                in1=ropeC[:, tt * 64:(tt + 1) * 64].unsqueeze(1).to_broadcast([128, 10, 64]), op=ALU.mult))(),
                reads=[b_qg, b_rope], writes=[b_Aa])

            def v4(ap):
                return ap.rearrange("p (a r h s) -> p a r h s", a=10, r=2, h=2, s=16)

            def s4(tt, h):
                return ropeS[:, tt * 64:(tt + 1) * 64].rearrange("p (r h s) -> p r h s", r=2, h=2, s=16)[:, :, h, :] \
                    .unsqueeze(1).to_broadcast([128, 10, 2, 16])
            B.op(DVE, (lambda tt=tt: lambda e: e.tensor_tensor(out=v4(Bt)[:, :, :, 0, :], in0=v4(qg)[:, :, :, 1, :],
                                                               in1=s4(tt, 0), op=ALU.mult))(),
                 reads=[b_qg, b_rope], writes=[b_Bt])
            B.op(DVE, (lambda tt=tt: lambda e: e.tensor_tensor(out=v4(Bt)[:, :, :, 1, :], in0=v4(qg)[:, :, :, 0, :],
                                                               in1=s4(tt, 1), op=ALU.mult))(),
                 reads=[b_qg, b_rope], writes=[b_Bt])
            B.op(DVE, lambda e: e.tensor_tensor(
                out=qrope.rearrange("p (c e d) -> p e c d", c=4, e=2, d=64),
                in0=Aa[:, 0:512].rearrange("p (e c d) -> p e c d", e=2, c=4, d=64),
                in1=Bt[:, 0:512].rearrange("p (e c d) -> p e c d", e=2, c=4, d=64), op=ALU.add),
                reads=[b_Aa, b_Bt], writes=[b_qrope])
            B.op(POOL, lambda e: e.tensor_tensor(out=krope, in0=Aa[:, 512:640], in1=Bt[:, 512:640], op=ALU.add),
                 reads=[b_Aa, b_Bt], writes=[b_krope])
            B.wait_only(PE, reads=[b_qrope, b_krope], writes=[bankb[4]])
            for c in range(4):
                B.raw(PE, (lambda c=c: lambda e: e.transpose(tq[:, c * 128:(c + 1) * 128], qrope[:, c * 128:(c + 1) * 128], ident_bf))())
            B.op(PE, lambda e: e.transpose(tq[:, 512:640], krope, ident_bf), reads=[b_qrope, b_krope], writes=[bankb[4]])
            B.op(ACT, (lambda tt=tt: lambda e: e.activation(out=QT3[:, :, tt * 128:(tt + 1) * 128], in_=v3(tq[:, 0:512], 4), func=AF.Identity))(),
                 reads=[bankb[4]], writes=[b_QT])
            B.op(ACT, (lambda tt=tt: lambda e: e.activation(out=KT[:, tt * 128:(tt + 1) * 128], in_=tq[:, 512:640], func=AF.Identity))(),
                 reads=[bankb[4]], writes=[b_KT])
            B.op(ACT, (lambda tt=tt: lambda e: e.activation(
                out=VA.rearrange("p (k a b) -> p k a b", a=3, b=64)[:, tt, 0:3:2, :], in_=v3(pj[1][:, 128:256], 2), func=AF.Identity))(),
                reads=[bankb[2]], writes=[b_VA])
            us = tt % 4
            B.op(ACT, (lambda us=us: lambda e: e.activation(out=upool[us][:, 0:256], in_=pj[1][:, 256:512], func=AF.Identity))(),
                 reads=[bankb[2]], writes=[b_up[us]])
            B.op(ACT, (lambda us=us: lambda e: e.activation(out=upool[us][:, 256:512], in_=pj[2][:, 0:256], func=AF.Identity))(),
                 reads=[bankb[3]], writes=[b_up[us]])
            if tt >= 1:
                pool_tile(tt - 1)
        pool_tile(NTT - 1)
        if dbg:
            s_d1 = B.new_dma_sem("s_d1")
            B.barrier()
            B.op(SP, lambda e: e.dma_start(out=d_qt, in_=QT), dma=s_d1)
            B.op(SP, lambda e: e.dma_start(out=d_kt, in_=KT), dma=B.new_dma_sem("s_d2"))
            B.op(SP, lambda e: e.dma_start(out=d_v, in_=VA), dma=B.new_dma_sem("s_d3"))
            B.op(SP, lambda e: e.dma_start(out=d_pt, in_=PT), dma=B.new_dma_sem("s_d4"))
        B.barrier()

        A.off = ab_end
        wout_sb = A.alloc(8192, BF16)
        attT = [A.alloc(2048, BF16) for _ in range(2)]
        pt = [A.alloc(512, BF16) for _ in range(4)]
        xin = [A.alloc(1024, F32) for _ in range(2)]
        x1o = [A.alloc(1024, F32) for _ in range(2)]
        rsum = [A.alloc(512, F32) for _ in range(2)]
        bcs = [A.alloc(512, F32) for _ in range(2)]
        b_wout = Buf("wout")
        b_att = [Buf("att%d" % i) for i in range(2)]
        b_pt = [Buf("pt%d" % i) for i in range(4)]
        b_xin = [Buf("xin%d" % i) for i in range(2)]
        b_x1o = [Buf("x1o%d" % i) for i in range(2)]
        b_rsum = [Buf("rsum%d" % i) for i in range(2)]
        b_bcs = [Buf("bcs%d" % i) for i in range(2)]
        s_wout = B.new_dma_sem("s_wout")
        s_xin = [B.new_dma_sem("s_xin%d" % i) for i in range(2)]
        s_x1o = [B.new_dma_sem("s_x1o%d" % i) for i in range(2)]
        B.op(SP, lambda e: e.dma_start(out=wout_sb, in_=w_out_bf), writes=[b_wout], dma=s_wout)
        sS = [0, 1, 2]
        sO = [3, 4]
        sB = 5
        sP = [6, 7]
        items = [(qs, c, e_, kt) for qs in range(8) for c in range(4) for e_ in range(2) for kt in range(NTT)]
        NI = len(items)
        pending = []
        xcnt = [0]

        def emit_qk(i):
            qs, c, e_, kt = items[i]
            bk = sS[i % 3]
            B.op(PE, lambda e: e.matmul(banks[bk][:, :], lhsT=KT[e_ * 64:(e_ + 1) * 64, kt * 128:(kt + 1) * 128],
                                        rhs=QT3[e_ * 64:(e_ + 1) * 64, c, qs * 512:(qs + 1) * 512], start=True, stop=True),
                 writes=[bankb[bk]])

        def emit_exp(i):
            bk = sS[i % 3]
            B.op(ACT, lambda e: e.activation(out=pt[i % 4], in_=banks[bk][:, :], func=AF.Exp, bias=negM[:, 0:1], scale=1.0),
                 reads=[bankb[bk]], writes=[b_pt[i % 4]])

        def emit_pv(i):
            qs, c, e_, kt = items[i]
            hidx = i // NTT
            bk = sO[hidx % 2]
            if e_ == 0:
                lhsT = VA3[:, kt, 0:65]
                o_ = banks[bk][0:65, :]
            else:
                lhsT = VA3[:, kt, 64:192]
                o_ = banks[bk][:, :]
            wr = [bankb[bk]] if kt in (0, NTT - 1) else []
            B.op(PE, lambda e: e.matmul(o_, lhsT=lhsT, rhs=pt[i % 4], start=(kt == 0), stop=(kt == NTT - 1)),
                 reads=[b_pt[i % 4]], writes=wr)

        def outproj(qs):
            ab = qs % 2
            for j in range(4):
                xsl = xcnt[0] % 2
                xcnt[0] += 1
                r0 = qs * 512 + j * 128
                B.op(SP, lambda e, r0=r0, xsl=xsl: e.dma_start(out=xin[xsl], in_=x[r0:r0 + 128, :]),
                     writes=[b_xin[xsl]], dma=s_xin[xsl])
                for half in range(2):
                    bk = sP[half]
                    B.wait_only(PE, reads=[b_att[ab], b_wout], writes=[bankb[bk]])
                    for cc in range(8):
                        if cc < 4:
                            lhsT = attT[ab][:, cc * 512 + j * 128:cc * 512 + (j + 1) * 128]
                        else:
                            lhsT = PT3[:, cc - 4, r0:r0 + 128]
                        fn = (lambda lhsT=lhsT, cc=cc, half=half, bk=bk: lambda e: e.matmul(
                            banks[bk][:, :], lhsT=lhsT, rhs=wout_sb[:, cc * 1024 + half * 512:cc * 1024 + (half + 1) * 512],
                            start=(cc == 0), stop=(cc == 7)))()
                        if cc < 7:
                            B.raw(PE, fn)
                        else:
                            B.op(PE, fn, reads=[b_att[ab], b_wout], writes=[bankb[bk]])
                    B.op(DVE, lambda e, half=half, bk=bk, xsl=xsl: e.tensor_tensor(
                        out=x1o[xsl][:, half * 512:(half + 1) * 512], in0=banks[bk][:, :],
                        in1=xin[xsl][:, half * 512:(half + 1) * 512], op=ALU.add),
                        reads=[bankb[bk], b_xin[xsl]], writes=[b_x1o[xsl]])
                B.op(POOL, lambda e, r0=r0, xsl=xsl: e.dma_start(out=x1s[1 + r0:1 + r0 + 128, :], in_=x1o[xsl]),
                     reads=[b_x1o[xsl]], dma=s_x1o[xsl])

        def head_post(i):
            qs, c, e_, kt = items[i]
            hidx = i // NTT
            bk = sO[hidx % 2]
            hs = hidx % 2
            srow = 64 if e_ == 0 else 0
            r0, r1 = e_ * 64, (e_ + 1) * 64
            B.op(DVE, lambda e: e.reciprocal(out=rsum[hs][srow:srow + 1, :], in_=banks[bk][srow:srow + 1, :]),
                 reads=[bankb[bk]], writes=[b_rsum[hs]])

            def pe_bc():
                B.op(PE, lambda e: e.matmul(banks[sB][:, :], lhsT=ones_f[srow:srow + 1, 0:128], rhs=rsum[hs][srow:srow + 1, :],
                                            start=True, stop=True), reads=[b_rsum[hs]], writes=[bankb[sB]])
                B.op(DVE, lambda e: e.tensor_copy(out=bcs[hs][r0:r1, :], in_=banks[sB][r0:r1, :]),
                     reads=[bankb[sB]], writes=[b_bcs[hs]])
                B.op(DVE, lambda e: e.tensor_tensor(out=attT[qs % 2][r0:r1, c * 512:(c + 1) * 512], in0=banks[bk][r0:r1, :],
                                                    in1=bcs[hs][r0:r1, :], op=ALU.mult),
                     reads=[bankb[bk], b_bcs[hs]], writes=[b_att[qs % 2]])
                if c == 3 and e_ == 1:
                    pending.append((i + 8, lambda: outproj(qs)))
            pending.append((i + 5, pe_bc))

        emit_qk(0)
        emit_qk(1)
        for i in range(NI):
            emit_exp(i)
            if i + 2 < NI:
                emit_qk(i + 2)
            emit_pv(i)
            if items[i][3] == NTT - 1:
                head_post(i)
            while pending and pending[0][0] <= i:
                pending.pop(0)[1]()
        while pending:
            pending.pop(0)[1]()
        if dbg:
            B.barrier()
            B.op(SP, lambda e: e.dma_start(out=d_x1, in_=x1s), dma=B.new_dma_sem("s_d5"))
        B.barrier()

        A.off = base
        wdn = A.alloc(NCH * 1024, BF16)
        actT = A.alloc(NCH * 512, BF16)
        h2T = [A.alloc(8 * 512, BF16) for _ in range(2)]
        x1u = [A.alloc(1024, F32) for _ in range(2)]
        st2 = [A.alloc(16, F32) for _ in range(2)]
        junk2 = A.alloc(1024, BF16)
        htmp2 = A.alloc(1024, F32)
        h2b = [A.alloc(1024, BF16) for _ in range(2)]
        NW = 4
        wup = [A.alloc(2048, BF16) for _ in range(NW)]
        Tg = [A.alloc(512, F32) for _ in range(2)]
        Tv = [A.alloc(512, F32) for _ in range(2)]
        Sg = [A.alloc(512, F32) for _ in range(2)]
        xr = [A.alloc(1024, F32) for _ in range(2)]
        outt = [A.alloc(1024, F32) for _ in range(2)]
        actT3 = actT.rearrange("p (c t) -> p c t", c=NCH)
        b_wdn = Buf("wdn")
        b_actT = [Buf("actT%d" % i) for i in range(NCH)]
        b_h2T = [Buf("h2T%d" % i) for i in range(2)]
        b_x1u = [Buf("x1u%d" % i) for i in range(2)]
        b_st2 = [Buf("st2%d" % i) for i in range(2)]
        b_junk2, b_htmp2 = Buf("junk2"), Buf("htmp2")
        b_h2b = [Buf("h2b%d" % i) for i in range(2)]
        b_wup = [Buf("wup%d" % i) for i in range(NW)]
        b_Tg = [Buf("Tg%d" % i) for i in range(2)]
        b_Tv = [Buf("Tv%d" % i) for i in range(2)]
        b_Sg = [Buf("Sg%d" % i) for i in range(2)]
        b_xr = [Buf("xr%d" % i) for i in range(2)]
        b_outt = [Buf("outt%d" % i) for i in range(2)]
        s_wdn = B.new_dma_sem("s_wdn")
        s_x1u = [B.new_dma_sem("s_x1u%d" % i) for i in range(2)]
        s_wup = [B.new_dma_sem("s_wup%d" % i) for i in range(NW)]
        s_xr = [B.new_dma_sem("s_xr%d" % i) for i in range(2)]
        s_outt = [B.new_dma_sem("s_outt%d" % i) for i in range(2)]
        B.op(SP, lambda e: e.dma_start(out=wdn, in_=w_down_bf), writes=[b_wdn], dma=s_wdn)
        tp2 = banks[0][:, :].bitcast(BF16)
        Gb = [1, 2]
        Vb = [3, 4]
        Yb = [5, 6]
        stages = [(510 * i, 510) for i in range(8)] + [(4080, 16)]
        NS = len(stages)
        pcnt = [0]
        pe_defer = []

        def prep_tile(si, j):
            o, n = stages[si]
            N = n + 2
            m = min(128, N - j * 128)
            hs_ = si % 2
            k = pcnt[0] % 2
            pcnt[0] += 1
            r0 = o + j * 128
            h2T3 = h2T[hs_].rearrange("p (k t) -> p k t", k=8)
            B.op(SP, lambda e: e.dma_start(out=x1u[k][0:m, :], in_=x1s[r0:r0 + m, :]), writes=[b_x1u[k]], dma=s_x1u[k])
            B.op(ACT, lambda e: e.activation(out=junk2[0:m, :], in_=x1u[k][0:m, :], func=AF.Square, accum_out=st2[k][0:m, 0:1]),
                 reads=[b_x1u[k]], writes=[b_junk2, b_st2[k]])
            B.op(ACT, lambda e: e.activation(out=st2[k][0:m, 1:2], in_=st2[k][0:m, 0:1], func=AF.Sqrt, scale=1.0 / D, bias=EPS),
                 reads=[b_st2[k]], writes=[b_st2[k]])
            B.op(DVE, lambda e: e.reciprocal(out=st2[k][0:m, 2:3], in_=st2[k][0:m, 1:2]), reads=[b_st2[k]], writes=[b_st2[k]])
            B.op(DVE, lambda e: e.scalar_tensor_tensor(out=htmp2[0:m, :], in0=x1u[k][0:m, :], scalar=st2[k][0:m, 2:3],
                                                       in1=a2[0:m, :], op0=ALU.mult, op1=ALU.mult),
                 reads=[b_x1u[k], b_st2[k]], writes=[b_htmp2])
            B.op(POOL, lambda e: e.tensor_tensor(out=h2b[k][0:m, :], in0=htmp2[0:m, :], in1=sh2[0:m, :], op=ALU.add),
                 reads=[b_htmp2], writes=[b_h2b[k]])

            def pe_part():
                B.wait_only(PE, reads=[b_h2b[k]], writes=[bankb[0]])
                for kc in range(8):
                    fn = (lambda kc=kc: lambda e: e.transpose(tp2[:, kc * 128:kc * 128 + m], h2b[k][0:m, kc * 128:(kc + 1) * 128],
                                                              ident_bf[0:m, 0:m]))()
                    if kc < 7:
                        B.raw(PE, fn)
                    else:
                        B.op(PE, fn, reads=[b_h2b[k]], writes=[bankb[0]])
                B.op(ACT, lambda e: e.activation(out=h2T3[:, :, j * 128:j * 128 + m],
                                                 in_=tp2.rearrange("p (k t) -> p k t", k=8)[:, :, 0:m], func=AF.Identity),
                     reads=[bankb[0]], writes=[b_h2T[hs_]])
                if si == 0 and j == 0:
                    B.op(POOL, lambda e: e.memset(h2T3[:, :, 0:1], 0.0), writes=[b_h2T[hs_]])
                if si == NS - 1 and j * 128 + m == N:
                    B.op(POOL, lambda e: e.memset(h2T3[:, :, N - 1:N], 0.0), writes=[b_h2T[hs_]])
            return pe_part

        def ntiles(si):
            return (stages[si][1] + 2 + 127) // 128

        wcnt = [0]
        wq = []

        def issue_wup(c):
            k = wcnt[0] % NW
            wcnt[0] += 1
            B.op(SP, lambda e: e.dma_start(out=wup[k], in_=w_up_bf[:, c * 2048:(c + 1) * 2048]), writes=[b_wup[k]], dma=s_wup[k])
            wq.append(k)

        for j in range(ntiles(0)):
            prep_tile(0, j)()
        allc = [(si, c) for si in range(NS) for c in range(NCH)]
        for q_ in range(min(NW - 1, len(allc))):
            issue_wup(allc[q_][1])
        ccnt = [0]
        ycnt = [0]
        ocnt = [0]

        def ffn_stage(si):
            o, n = stages[si]
            N = n + 2
            hs_ = si % 2
            h2T3 = h2T[hs_].rearrange("p (k t) -> p k t", k=8)
            nprep = ntiles(si + 1) if si + 1 < NS else 0
            prep_at = {3 + 4 * jj: jj for jj in range(nprep)}
            for c in range(NCH):
                gi = si * NCH + c
                if gi + NW - 1 < len(allc):
                    issue_wup(allc[gi + NW - 1][1])
                k = wq.pop(0)
                b2 = ccnt[0] % 2
                ccnt[0] += 1
                for which, bkl, off in ((0, Gb, 0), (1, Vb, 128)):
                    bk = bkl[b2]
                    B.wait_only(PE, reads=[b_wup[k], b_h2T[hs_]], writes=[bankb[bk]])
                    for kc in range(8):
                        fn = (lambda kc=kc, bk=bk, off=off, k=k: lambda e: e.matmul(
                            banks[bk][:, 0:N], lhsT=wup[k][:, kc * 256 + off:kc * 256 + off + 128],
                            rhs=h2T3[:, kc, 0:N], start=(kc == 0), stop=(kc == 7)))()
                        if kc < 7:
                            B.raw(PE, fn)
                        else:
                            B.op(PE, fn, reads=[b_wup[k], b_h2T[hs_]], writes=[bankb[bk]])
                for which, bkl, Tl, bTl, cc in ((0, Gb, Tg, b_Tg, c), (1, Vb, Tv, b_Tv, NCH + c)):
                    bk = bkl[b2]
                    Tt = Tl[b2]
                    bT = bTl[b2]
                    B.op(ACT, lambda e, bk=bk, Tt=Tt, cc=cc: e.activation(
                        out=Tt[:, 0:n], in_=banks[bk][:, 0:n], func=AF.Identity,
                        scale=convp[:, cc * 4:cc * 4 + 1], bias=convp[:, cc * 4 + 3:cc * 4 + 4]),
                        reads=[bankb[bk]], writes=[bT])
                    B.op(DVE, lambda e, bk=bk, Tt=Tt, cc=cc: e.scalar_tensor_tensor(
                        out=Tt[:, 0:n], in0=banks[bk][:, 1:n + 1], scalar=convp[:, cc * 4 + 1:cc * 4 + 2], in1=Tt[:, 0:n],
                        op0=ALU.mult, op1=ALU.add), reads=[bankb[bk], bT], writes=[bT])
                    B.op(DVE, lambda e, bk=bk, Tt=Tt, cc=cc: e.scalar_tensor_tensor(
                        out=Tt[:, 0:n], in0=banks[bk][:, 2:n + 2], scalar=convp[:, cc * 4 + 2:cc * 4 + 3], in1=Tt[:, 0:n],
                        op0=ALU.mult, op1=ALU.add), reads=[bankb[bk], bT], writes=[bT])
                B.op(ACT, lambda e, b2=b2: e.activation(out=Sg[b2][:, 0:n], in_=Tg[b2][:, 0:n], func=AF.Silu),
                     reads=[b_Tg[b2]], writes=[b_Sg[b2]])
                B.op(DVE, lambda e, b2=b2, c=c: e.tensor_tensor(out=actT3[:, c, 0:n], in0=Sg[b2][:, 0:n], in1=Tv[b2][:, 0:n],
                                                               op=ALU.mult),
                     reads=[b_Sg[b2], b_Tv[b2]], writes=[b_actT[c]])
                if c in prep_at:
                    pe_defer.append((c + 2, prep_tile(si + 1, prep_at[c])))
                while pe_defer and pe_defer[0][0] <= c:
                    pe_defer.pop(0)[1]()
            while pe_defer:
                pe_defer.pop(0)[1]()
            for j in range((n + 127) // 128):
                m = min(128, n - j * 128)
                xk = ocnt[0] % 2
                ocnt[0] += 1
                r0 = o + j * 128
                B.op(SP, lambda e, xk=xk, r0=r0, m=m: e.dma_start(out=xr[xk][0:m, :], in_=x1s[1 + r0:1 + r0 + m, :]),
                     writes=[b_xr[xk]], dma=s_xr[xk])
                for half in range(2):
                    yk = Yb[ycnt[0] % 2]
                    ycnt[0] += 1
                    B.wait_only(PE, reads=b_actT + [b_wdn], writes=[bankb[yk]])
                    for c in range(NCH):
                        fn = (lambda c=c, yk=yk, half=half, j=j, m=m: lambda e: e.matmul(
                            banks[yk][0:m, :], lhsT=actT3[:, c, j * 128:j * 128 + m],
                            rhs=wdn[:, c * 1024 + half * 512:c * 1024 + (half + 1) * 512],
                            start=(c == 0), stop=(c == NCH - 1)))()
                        if c < NCH - 1:
                            B.raw(PE, fn)
                        else:
                            B.op(PE, fn, reads=b_actT + [b_wdn], writes=[bankb[yk]])
                    B.op(DVE, lambda e, yk=yk, xk=xk, half=half, m=m: e.tensor_tensor(
                        out=outt[xk][0:m, half * 512:(half + 1) * 512], in0=banks[yk][0:m, :],
                        in1=xr[xk][0:m, half * 512:(half + 1) * 512], op=ALU.add),
                        reads=[bankb[yk], b_xr[xk]], writes=[b_outt[xk]])
                B.op(POOL, lambda e, xk=xk, r0=r0, m=m: e.dma_start(out=out[r0:r0 + m, :], in_=outt[xk][0:m, :]),
                     reads=[b_outt[xk]], dma=s_outt[xk])
        for si in range(NS):
            ffn_stage(si)
        B.barrier()
        B.emit()
    return nc


def _rope_tables():
    t = np.arange(T)
    row = (t // 64).astype(np.float32)
    col = (t % 64).astype(np.float32)
    inv = (1.0 / (np.float32(10000.0) ** (np.arange(0, 32, 2, dtype=np.float32) / np.float32(32)))).astype(np.float32)
    ar = (row[:, None] * inv[None, :]).astype(np.float32)
    ac = (col[:, None] * inv[None, :]).astype(np.float32)
    cr, sr, cc, sc = np.cos(ar), np.sin(ar), np.cos(ac), np.sin(ac)
    C = np.concatenate([cr, cr, cc, cc], axis=1).astype(np.float32)
    S = np.concatenate([-sr, sr, -sc, sc], axis=1).astype(np.float32)
    def lay(a):
        return np.ascontiguousarray(a.reshape(NTT, 128, 64).transpose(1, 0, 2).reshape(128, NTT * 64))
    return lay(C), lay(S)


def _band_tables():
    band = np.zeros((128, 4, 5, 128), np.float32)
    for g, w in enumerate((2, 4, 8, 16)):
        def blk(t_out0, t_src0):
            M = np.zeros((128, 128), np.float32)
            for i in range(128):
                tt = t_out0 + i
                lo = max(tt - w // 2, 0)
                hi = min(tt + w // 2 - 1, T - 1)
                cnt = hi - lo + 1
                for ts in range(lo, hi + 1):
                    j = ts - t_src0
                    if 0 <= j < 128:
                        M[j, i] += 1.0 / cnt
                j = tt - t_src0
                if 0 <= j < 128:
                    M[j, i] -= 1.0
            return M
        band[:, g, 0] = blk(1280, 1152)
        band[:, g, 1] = blk(1280, 1280)
        band[:, g, 2] = blk(1280, 1408)
        band[:, g, 3] = blk(0, 0)
        band[:, g, 4] = blk(T - 128, T - 128)
    return np.ascontiguousarray(band.reshape(128, 2560))


_CACHE = {}


def _shared_inputs(w_ada, b_ada, norm1_g, w_in, q_norm_g, k_norm_g, w_pool, pool_scale, w_out, norm2_g,
                   w_up, conv_w, conv_b, w_down):
    f = np.float32
    d = {}
    d["w_ada_l"] = np.ascontiguousarray(
        np.asarray(w_ada[0], f).reshape(8, 128, 12, 512).transpose(1, 2, 0, 3).reshape(128, 12 * 4096))
    d["b_ada"] = np.ascontiguousarray(np.asarray(b_ada[0], f).reshape(1, 6144))
    d["n1g"] = np.ascontiguousarray(np.asarray(norm1_g[0], f).reshape(1, D))
    d["n2g"] = np.ascontiguousarray(np.asarray(norm2_g[0], f).reshape(1, D))
    d["w_in_l"] = np.ascontiguousarray(np.asarray(w_in[0], f).reshape(8, 128, 1280).transpose(1, 0, 2).reshape(128, 8 * 1280))
    d["gvec"] = np.ascontiguousarray(
        np.concatenate([np.tile(np.asarray(q_norm_g[0], f), 8), np.tile(np.asarray(k_norm_g[0], f), 2)]).reshape(1, 640))
    d["w_pool_l"] = np.ascontiguousarray(np.asarray(w_pool[0], f).transpose(1, 0, 2).reshape(128, 512))
    d["pscale_l"] = np.ascontiguousarray(np.asarray(pool_scale[0], f).reshape(4, 128).T)
    wo = np.asarray(w_out[0], f)
    chunks = []
    for c in range(4):
        chunks.append(np.concatenate([wo[c * 64:(c + 1) * 64], wo[(4 + c) * 64:(5 + c) * 64]], axis=0))
    for g in range(4):
        chunks.append(wo[512 + g * 128:512 + (g + 1) * 128])
    d["w_out_l"] = np.ascontiguousarray(np.stack(chunks, axis=1).reshape(128, 8 * 1024))
    wu = np.asarray(w_up[0], f).reshape(8, 128, 2, NCH, 128)
    d["w_up_l"] = np.ascontiguousarray(wu.transpose(1, 3, 0, 2, 4).reshape(128, NCH * 2048))
    cw = np.asarray(conv_w[0], f)
    cb = np.asarray(conv_b[0], f)
    cp = np.stack([cw[0], cw[1], cw[2], cb], axis=1).reshape(44, 128, 4).transpose(1, 0, 2)
    d["convp"] = np.ascontiguousarray(cp.reshape(128, 176))
    d["w_down_l"] = np.ascontiguousarray(np.asarray(w_down[0], f).reshape(NCH, 128, 1024).transpose(1, 0, 2).reshape(128, NCH * 1024))
    if "rope" not in _CACHE:
        _CACHE["rope"] = _rope_tables()
        _CACHE["band"] = _band_tables()
    d["ropeC"], d["ropeS"] = _CACHE["rope"]
    d["band"] = _CACHE["band"]
    d["ident"] = np.eye(128, dtype=f)
    return d


def kernel(x, c, w_ada, b_ada, norm1_g, w_in, q_norm_g, k_norm_g, w_pool, pool_scale,
           w_out, norm2_g, w_up, conv_w, conv_b, w_down):
    x = np.asarray(x, np.float32)
    c = np.asarray(c, np.float32)
    shared = _shared_inputs(w_ada, b_ada, norm1_g, w_in, q_norm_g, k_norm_g, w_pool, pool_scale, w_out, norm2_g,
                            w_up, conv_w, conv_b, w_down)
    if "nc" not in _CACHE:
        _CACHE["nc"] = build_program()
    nc = _CACHE["nc"]
    in_maps = []
    for b in range(8):
        m = dict(shared)
        m["x"] = np.ascontiguousarray(x[b])
        m["c_l"] = np.ascontiguousarray(c[b].reshape(8, 128).T)
        in_maps.append(m)
    res = run_bass_kernel_spmd(nc, in_maps, core_ids=list(range(8)))
    return np.stack([np.asarray(r["out"], np.float32) for r in res.results], axis=0)
```

```python
import math
import contextlib
import numpy as np
import concourse.bass as bass
import concourse.mybir as mybir
from concourse.bass_utils import run_bass_kernel_spmd

F32 = mybir.dt.float32
BF16 = mybir.dt.bfloat16
AF = mybir.ActivationFunctionType
ALU = mybir.AluOpType
AX = mybir.AxisListType

T = 4096
D = 1024
NTT = 32
DFF = 2816
NCH = 22
EPS = 1e-6
ARENA_BYTES = 204800

PE, ACT, DVE, POOL, SP = "tensor", "scalar", "vector", "gpsimd", "sync"
ENGS = [PE, ACT, DVE, POOL, SP]


class Sem:
    def __init__(self, h, idx):
        self.h = h
        self.idx = idx
        self.count = 0


class Buf:
    def __init__(self, name, excl=False):
        self.name = name
        self.excl = excl
        self.w = None
        self.r = {}


class Builder:
    def __init__(self, nc, stack):
        self.nc = nc
        self.stack = stack
        self.nsem = 0
        self.q = {e: [] for e in ENGS}
        self.esem = {e: self.new_sem("e_" + e) for e in ENGS}
        self.waited = {e: {} for e in ENGS}
        self.dma_sems = []

    def new_sem(self, name):
        h = self.stack.enter_context(self.nc.semaphore(name))
        s = Sem(h, self.nsem)
        self.nsem += 1
        return s

    def new_dma_sem(self, name):
        s = self.new_sem(name)
        self.dma_sems.append(s)
        return s

    def _collect(self, eng, reads, writes):
        evs = []
        for b in reads:
            if b.w is not None:
                evs.append(b.w)
            if b.excl:
                for ev in b.r.values():
                    if ev[2] != eng:
                        evs.append(ev)
        for b in writes:
            if b.w is not None and b.w[2] != eng:
                evs.append(b.w)
            for ev in b.r.values():
                if ev[2] != eng:
                    evs.append(ev)
        out = []
        for (s, v, pe) in evs:
            if eng == PE and pe == PE:
                continue
            if self.waited[eng].get(s.idx, 0) < v:
                self.waited[eng][s.idx] = v
                out.append((s.h, v))
        return out

    def wait_only(self, eng, reads=(), writes=()):
        w = self._collect(eng, reads, writes)
        if w:
            self.q[eng].append((w, None, None, 0))

    def raw(self, eng, fn):
        self.q[eng].append(([], fn, None, 0))

    def op(self, eng, fn, reads=(), writes=(), dma=None, regreads=True):
        w = self._collect(eng, reads, writes)
        if dma is None:
            s = self.esem[eng]
            s.count += 1
            inc = 1
        else:
            s = dma
            s.count += 16
            inc = 16
        ev = (s, s.count, eng if dma is None else "dma")
        self.q[eng].append((w, fn, s.h, inc))
        if regreads:
            for b in reads:
                b.r[s.idx] = ev
        for b in writes:
            b.w = ev
            b.r = {}
        return ev

    def barrier(self):
        for e in ENGS:
            w = []
            for e2 in ENGS:
                s = self.esem[e2]
                if s.count > 0 and self.waited[e].get(s.idx, 0) < s.count:
                    self.waited[e][s.idx] = s.count
                    w.append((s.h, s.count))
            for s in self.dma_sems:
                if s.count > 0 and self.waited[e].get(s.idx, 0) < s.count:
                    self.waited[e][s.idx] = s.count
                    w.append((s.h, s.count))
            if w:
                self.q[e].append((w, None, None, 0))

    def check(self):
        vals = {}
        ptr = {e: 0 for e in ENGS}
        progress = True
        while progress:
            progress = False
            for e in ENGS:
                while ptr[e] < len(self.q[e]):
                    waits, fn, sem, inc = self.q[e][ptr[e]]
                    if all(vals.get(id(h), 0) >= v for (h, v) in waits):
                        if sem is not None:
                            vals[id(sem)] = vals.get(id(sem), 0) + inc
                        ptr[e] += 1
                        progress = True
                    else:
                        break
        stuck = {e: (ptr[e], len(self.q[e])) for e in ENGS if ptr[e] < len(self.q[e])}
        if stuck:
            names = {}
            for e in ENGS:
                names[id(self.esem[e].h)] = "e_" + e
            for i, sm in enumerate(self.dma_sems):
                names[id(sm.h)] = "dma%d" % i
            for e, (pp_, n) in stuck.items():
                waits = self.q[e][pp_][0]
                print("STUCK", e, pp_, n, [(names.get(id(h), "?"), v, vals.get(id(h), 0)) for (h, v) in waits])
            raise RuntimeError("deadlock in emitted program")

    def emit(self):
        self.check()
        with self.nc.Block() as block:
            def mk(name):
                def f(e):
                    for waits, fn, sem, inc in self.q[name]:
                        for (h, v) in waits:
                            e.wait_ge(h, v)
                        if fn is not None:
                            ins = fn(e)
                            if sem is not None:
                                ins.then_inc(sem, inc)
                return f
            block.tensor(mk(PE))
            block.scalar(mk(ACT))
            block.vector(mk(DVE))
            block.gpsimd(mk(POOL))
            block.sync(mk(SP))


class Arena:
    def __init__(self, t):
        self.t = t
        self.off = 0

    def alloc(self, nelem, dt):
        esz = 4 if dt == F32 else 2
        sz = (nelem * esz + 63) // 64 * 64
        o = self.off
        self.off += sz
        assert self.off <= ARENA_BYTES, ("arena overflow", self.off)
        ap = self.t[:, o // 2:(o + sz) // 2]
        if dt == F32:
            ap = ap.bitcast(F32)
        return ap[:, 0:nelem]


def build_program(dbg=False, stop_after=None):
    nc = bass.Bass("TRN2", target_bir_lowering=False)

    def din(name, shape, dt=F32):
        return nc.dram_tensor(name, shape, dt, kind="ExternalInput").ap()

    x = din("x", [T, D])
    c_l = din("c_l", [128, 8])
    w_ada = din("w_ada_l", [128, 12 * 4096])
    b_ada = din("b_ada", [1, 6144])
    n1g = din("n1g", [1, D])
    n2g = din("n2g", [1, D])
    w_in = din("w_in_l", [128, 8 * 1280])
    gvec = din("gvec", [1, 640])
    w_pool = din("w_pool_l", [128, 512])
    pscale_d = din("pscale_l", [128, 4])
    w_out = din("w_out_l", [128, 8 * 1024])
    w_up = din("w_up_l", [128, NCH * 2048])
    convp_d = din("convp", [128, 176])
    w_down = din("w_down_l", [128, NCH * 1024])
    ropeC_d = din("ropeC", [128, 2048])
    ropeS_d = din("ropeS", [128, 2048])
    band_d = din("band", [128, 2560])
    ident_d = din("ident", [128, 128])
    out = nc.dram_tensor("out", [T, D], F32, kind="ExternalOutput").ap()
    w_up_bf = nc.dram_tensor("w_up_bf", [128, NCH * 2048], BF16, kind="Internal").ap()
    w_out_bf = nc.dram_tensor("w_out_bf", [128, 8192], BF16, kind="Internal").ap()
    w_down_bf = nc.dram_tensor("w_down_bf", [128, NCH * 1024], BF16, kind="Internal").ap()
    x1s = nc.dram_tensor("x1s", [T + 2, D], F32, kind="Internal").ap()
    if dbg:
        d_qt = nc.dram_tensor("d_qt", [128, 4 * T], BF16, kind="ExternalOutput").ap()
        d_kt = nc.dram_tensor("d_kt", [128, T], BF16, kind="ExternalOutput").ap()
        d_v = nc.dram_tensor("d_v", [128, NTT * 192], BF16, kind="ExternalOutput").ap()
        d_pt = nc.dram_tensor("d_pt", [128, 4 * T], BF16, kind="ExternalOutput").ap()
        d_mod = nc.dram_tensor("d_mod", [128, 6144], F32, kind="ExternalOutput").ap()
        d_x1 = nc.dram_tensor("d_x1", [T + 2, D], F32, kind="ExternalOutput").ap()

    with contextlib.ExitStack() as stack:
        arena_t = stack.enter_context(nc.sbuf_tensor("arena", [128, ARENA_BYTES // 2], BF16))
        ps_all = stack.enter_context(nc.psum_tensor("ps_all", [128, 4096], F32))
        banks = [ps_all[:, i * 512:(i + 1) * 512] for i in range(8)]
        bankb = [Buf("bank%d" % i, excl=True) for i in range(8)]
        B = Builder(nc, stack)
        A = Arena(arena_t)

        sh1 = A.alloc(1024, F32)
        a1 = A.alloc(1024, F32)
        sh2 = A.alloc(1024, F32)
        a2 = A.alloc(1024, F32)
        gv = A.alloc(640, F32)
        negM = A.alloc(16, F32)
        ident_bf = A.alloc(128, BF16)
        band_bf = A.alloc(2560, BF16)
        wpool_bf = A.alloc(512, BF16)
        pscale = A.alloc(16, F32)
        convp = A.alloc(176, F32)
        ones_f = A.alloc(128, F32)
        w_in_bf = A.alloc(8 * 1280, BF16)
        base = A.off
        QT = A.alloc(4 * T, BF16)
        KT = A.alloc(T, BF16)
        VA = A.alloc(NTT * 192, BF16)
        PT = A.alloc(4 * T, BF16)
        ab_end = A.off
        QT3 = QT.rearrange("p (c t) -> p c t", c=4)
        VA3 = VA.rearrange("p (k c) -> p k c", c=192)
        PT3 = PT.rearrange("p (g t) -> p g t", g=4)

        A.off = base
        badabc = A.alloc(6144, F32)
        modbc = A.alloc(6144, F32)
        n1gbc = A.alloc(1024, F32)
        n2gbc = A.alloc(1024, F32)
        csb = A.alloc(16, F32)
        cact = A.alloc(16, F32)
        cbc = A.alloc(1024, F32)
        mq = A.alloc(16, F32)
        wada_s = [A.alloc(4096, F32) for _ in range(2)]
        stin = [A.alloc(2048, F32) for _ in range(2)]
        stout = [A.alloc(2048, BF16) for _ in range(2)]
        zrow = A.alloc(1024, F32)

        b_consts = Buf("consts")
        b_small = Buf("small")
        b_cbc = Buf("cbc")
        b_mod = Buf("mod")
        b_wada = [Buf("wada%d" % i) for i in range(2)]
        b_stin = [Buf("stin%d" % i) for i in range(2)]
        b_stout = [Buf("stout%d" % i) for i in range(2)]
        s_wada = [B.new_dma_sem("s_wada%d" % i) for i in range(2)]
        s_stin = [B.new_dma_sem("s_stin%d" % i) for i in range(2)]
        s_stout = [B.new_dma_sem("s_stout%d" % i) for i in range(2)]
        s_misc = B.new_dma_sem("s_misc")
        s_cast = B.new_dma_sem("s_cast")
        s_cast2 = B.new_dma_sem("s_cast2")

        small_loads = [
            (csb[:, 0:8], c_l),
            (badabc, b_ada.partition_broadcast(128)),
            (n1gbc, n1g.partition_broadcast(128)),
            (n2gbc, n2g.partition_broadcast(128)),
            (gv, gvec.partition_broadcast(128)),
            (pscale[:, 0:4], pscale_d),
            (convp, convp_d),
        ]
        for i, (o_, i_) in enumerate(small_loads):
            last = i == len(small_loads) - 1
            if last:
                B.op(SP, (lambda o_=o_, i_=i_: lambda e: e.dma_start(out=o_, in_=i_))(),
                     writes=[b_consts], dma=s_misc)
            else:
                s_misc.count += 16
                B.q[SP].append(([], (lambda o_=o_, i_=i_: lambda e: e.dma_start(out=o_, in_=i_))(), s_misc.h, 16))
        B.op(POOL, lambda e: e.dma_start(out=w_in_bf.rearrange("p (a b) -> p a b", b=1280),
                                         in_=w_in.rearrange("p (a b) -> p a b", b=1280)),
             writes=[Buf("win")], dma=s_cast)
        b_c2 = Buf("c2")
        s_cast2.count += 32
        B.q[POOL].append(([], lambda e: e.dma_start(out=ident_bf, in_=ident_d), s_cast2.h, 16))
        B.q[POOL].append(([], lambda e: e.dma_start(out=band_bf.rearrange("p (a b) -> p a b", b=1280), in_=band_d.rearrange("p (a b) -> p a b", b=1280)), s_cast2.h, 16))
        B.op(POOL, lambda e: e.dma_start(out=wpool_bf, in_=w_pool), writes=[b_c2], dma=s_cast2)
        s_cast3 = B.new_dma_sem("s_cast3")
        B.op(POOL, lambda e: e.dma_start(out=w_up_bf.rearrange("p (a b) -> p a b", b=2048),
                                         in_=w_up.rearrange("p (a b) -> p a b", b=2048)),
             writes=[Buf("wupbf")], dma=s_cast3)

        B.op(DVE, lambda e: e.memset(ones_f, 1.0), writes=[b_small])
        B.op(DVE, lambda e: e.memset(zrow, 0.0), writes=[b_small])
        B.op(ACT, lambda e: e.activation(out=cact[:, 0:8], in_=csb[:, 0:8], func=AF.Silu),
             reads=[b_consts], writes=[b_small])
        B.op(DVE, lambda e: e.tensor_copy(out=cbc.rearrange("p (a b) -> p a b", a=8),
                                          in_=cact[:, 0:8].unsqueeze(2).to_broadcast([128, 8, 128])),
             reads=[b_small], writes=[b_cbc])
        s_z = B.new_dma_sem("s_z")
        s_z2 = B.new_dma_sem("s_z2")
        B.op(SP, lambda e: e.dma_start(out=x1s[0:1, :], in_=zrow[0:1, :]), reads=[b_small], writes=[Buf("z0")], dma=s_z)
        B.op(SP, lambda e: e.dma_start(out=x1s[T + 1:T + 2, :], in_=zrow[0:1, :]), reads=[b_small], writes=[Buf("z1")], dma=s_z2)

        for nt in range(12):
            sl = nt % 2
            B.op(SP, (lambda nt=nt, sl=sl: lambda e: e.dma_start(out=wada_s[sl], in_=w_ada[:, nt * 4096:(nt + 1) * 4096]))(),
                 writes=[b_wada[sl]], dma=s_wada[sl])
            bk = nt % 2
            B.wait_only(PE, reads=[b_wada[sl], b_cbc], writes=[bankb[bk]])
            for kc in range(8):
                fn = (lambda kc=kc, sl=sl, bk=bk: lambda e: e.matmul(
                    banks[bk][:, :], lhsT=cbc[:, kc * 128:(kc + 1) * 128],
                    rhs=wada_s[sl][:, kc * 512:(kc + 1) * 512], start=(kc == 0), stop=(kc == 7)))()
                if kc < 7:
                    B.raw(PE, fn)
                else:
                    B.op(PE, fn, reads=[b_wada[sl], b_cbc], writes=[bankb[bk]])
            B.op(DVE, (lambda nt=nt, bk=bk: lambda e: e.tensor_tensor(
                out=modbc[:, nt * 512:(nt + 1) * 512], in0=banks[bk][:, :],
                in1=badabc[:, nt * 512:(nt + 1) * 512], op=ALU.add))(),
                reads=[bankb[bk], b_consts], writes=[b_mod])
        b_der = Buf("derived")
        B.op(DVE, lambda e: e.scalar_tensor_tensor(out=a1, in0=modbc[:, 1024:2048], scalar=1.0, in1=n1gbc,
                                                   op0=ALU.add, op1=ALU.mult), reads=[b_mod, b_consts], writes=[b_der])
        B.op(DVE, lambda e: e.scalar_tensor_tensor(out=a2, in0=modbc[:, 4096:5120], scalar=1.0, in1=n2gbc,
                                                   op0=ALU.add, op1=ALU.mult), reads=[b_mod, b_consts], writes=[b_der])
        B.op(DVE, lambda e: e.tensor_copy(out=sh1, in_=modbc[:, 0:1024]), reads=[b_mod], writes=[b_der])
        B.op(DVE, lambda e: e.tensor_copy(out=sh2, in_=modbc[:, 3072:4096]), reads=[b_mod], writes=[b_der])
        b_m = Buf("m")
        B.op(DVE, lambda e: e.tensor_reduce(out=mq[:, 0:1], in_=gv[:, 0:64], axis=AX.X, op=ALU.max,
                                            apply_absolute_value=True), reads=[b_consts], writes=[b_m])
        B.op(DVE, lambda e: e.tensor_reduce(out=mq[:, 1:2], in_=gv[:, 512:576], axis=AX.X, op=ALU.max,
                                            apply_absolute_value=True), reads=[b_consts], writes=[b_m])
        B.op(DVE, lambda e: e.tensor_tensor(out=mq[:, 2:3], in0=mq[:, 0:1], in1=mq[:, 1:2], op=ALU.mult),
             reads=[b_m], writes=[b_m])
        B.op(DVE, lambda e: e.tensor_scalar(out=negM[:, 0:1], in0=mq[:, 2:3], scalar1=-8.0, scalar2=None,
                                            op0=ALU.mult), reads=[b_m], writes=[b_m])
        B.op(DVE, lambda e: e.tensor_scalar(out=gv[:, 0:512], in0=gv[:, 0:512], scalar1=0.125, scalar2=None,
                                            op0=ALU.mult), reads=[b_m], writes=[b_consts])
        pieces = [(w_out, w_out_bf, i, 2048) for i in range(4)] + [(w_down, w_down_bf, i, 5120) for i in range(11)]
        for k, (src, dst, i, goff) in enumerate(pieces):
            sl = k % 2
            B.op(SP, (lambda src=src, i=i, sl=sl: lambda e: e.dma_start(out=stin[sl], in_=src[:, i * 2048:(i + 1) * 2048]))(),
                 writes=[b_stin[sl]], dma=s_stin[sl])
            B.op(DVE, (lambda sl=sl, goff=goff: lambda e: e.tensor_tensor(
                out=stout[sl].rearrange("p (a b) -> p a b", a=2),
                in0=stin[sl].rearrange("p (a b) -> p a b", a=2),
                in1=modbc[:, goff:goff + 1024].unsqueeze(1).to_broadcast([128, 2, 1024]), op=ALU.mult))(),
                reads=[b_stin[sl], b_mod], writes=[b_stout[sl]])
            B.op(POOL, (lambda dst=dst, i=i, sl=sl: lambda e: e.dma_start(out=dst[:, i * 2048:(i + 1) * 2048], in_=stout[sl]))(),
                 reads=[b_stout[sl]], dma=s_stout[sl])
        if dbg:
            s_d0 = B.new_dma_sem("s_d0")
            B.op(SP, lambda e: e.dma_start(out=d_mod, in_=modbc), reads=[b_mod], dma=s_d0)
        B.barrier()

        A.off = ab_end
        ropeC = A.alloc(2048, F32)
        ropeS = A.alloc(2048, F32)
        xs = [A.alloc(1024, F32) for _ in range(2)]
        junk = A.alloc(1024, BF16)
        st = [A.alloc(16, F32) for _ in range(2)]
        htmp = [A.alloc(1024, F32) for _ in range(2)]
        hb = [A.alloc(1024, BF16) for _ in range(2)]
        hT = [A.alloc(1024, BF16) for _ in range(2)]
        sq = A.alloc(640, F32)
        s10 = A.alloc(32, F32)
        qn = A.alloc(640, F32)
        qg = A.alloc(640, F32)
        Aa = A.alloc(640, F32)
        Bt = A.alloc(640, F32)
        qrope = A.alloc(512, BF16)
        krope = A.alloc(128, BF16)
        upool = [A.alloc(512, BF16) for _ in range(4)]
        pld = A.alloc(512, BF16)

        b_rope = Buf("rope")
        s_rope = B.new_dma_sem("s_rope")
        s_rope.count += 16
        B.q[SP].append(([], lambda e: e.dma_start(out=ropeC, in_=ropeC_d), s_rope.h, 16))
        B.op(SP, lambda e: e.dma_start(out=ropeS, in_=ropeS_d), writes=[b_rope], dma=s_rope)
        B.op(POOL, lambda e: e.memset(VA3[:, :, 64:128], 0.0), writes=[Buf("va0")])
        B.op(POOL, lambda e: e.memset(VA3[:, :, 64:65], 1.0), writes=[Buf("va1")])

        b_xs = [Buf("xs%d" % i) for i in range(2)]
        s_xs = [B.new_dma_sem("s_xs%d" % i) for i in range(2)]
        b_st = [Buf("st%d" % i) for i in range(2)]
        b_junk, b_sq, b_s10 = Buf("junk"), Buf("sq"), Buf("s10")
        b_htmp = [Buf("htmp0"), Buf("htmp1")]
        b_hb = [Buf("hb0"), Buf("hb1")]
        b_hT = [Buf("hT0"), Buf("hT1")]
        b_qn, b_qg, b_Aa, b_Bt, b_qrope, b_krope = Buf("qn"), Buf("qg"), Buf("Aa"), Buf("Bt"), Buf("qrope"), Buf("krope")
        b_up = [Buf("up%d" % i) for i in range(4)]
        b_pld = Buf("pld")
        b_QT, b_KT, b_VA, b_PT = Buf("QT"), Buf("KT"), Buf("VA"), Buf("PT")
        tpA = banks[0][:, :].bitcast(BF16)
        tq = banks[4][:, :].bitcast(BF16)
        pj = [banks[1], banks[2], banks[3]]
        pp, po = banks[5], banks[6]

        def v3(ap, a):
            return ap.rearrange("p (a b) -> p a b", a=a)

        def pool_tile(t):
            srcs = []
            if t > 0:
                srcs.append((t - 1, 0))
            srcs.append((t, 3 if t == 0 else (4 if t == NTT - 1 else 1)))
            if t < NTT - 1:
                srcs.append((t + 1, 2))
            rd = [b_up[s % 4] for (s, _) in srcs]
            B.wait_only(PE, reads=rd, writes=[bankb[5]])
            for g in range(4):
                for si, (s, kind) in enumerate(srcs):
                    fn = (lambda g=g, s=s, kind=kind, si=si: lambda e: e.matmul(
                        pp[:, g * 128:(g + 1) * 128], lhsT=upool[s % 4][:, g * 128:(g + 1) * 128],
                        rhs=band_bf[:, (g * 5 + kind) * 128:(g * 5 + kind + 1) * 128],
                        start=(si == 0), stop=(si == len(srcs) - 1)))()
                    if g == 3 and si == len(srcs) - 1:
                        B.op(PE, fn, reads=rd, writes=[bankb[5]])
                    else:
                        B.raw(PE, fn)
            B.op(ACT, lambda e: e.activation(out=pld, in_=pp[:, :], func=AF.Identity), reads=[bankb[5]], writes=[b_pld])
            B.wait_only(PE, reads=[b_pld], writes=[bankb[6]])
            for g in range(4):
                fn = (lambda g=g: lambda e: e.matmul(po[:, g * 128:(g + 1) * 128], lhsT=wpool_bf[:, g * 128:(g + 1) * 128],
                                                     rhs=pld[:, g * 128:(g + 1) * 128], start=True, stop=True))()
                if g == 3:
                    B.op(PE, fn, reads=[b_pld], writes=[bankb[6]])
                else:
                    B.raw(PE, fn)
            for g in range(4):
                B.op(ACT, (lambda g=g, t=t: lambda e: e.activation(
                    out=PT3[:, g, t * 128:(t + 1) * 128], in_=po[:, g * 128:(g + 1) * 128], func=AF.Identity,
                    scale=pscale[:, g:g + 1]))(), reads=[bankb[6]], writes=[b_PT])

        def pA_F1a(tt):
            sl = tt % 2
            B.op(SP, (lambda tt=tt, sl=sl: lambda e: e.dma_start(out=xs[sl], in_=x[tt * 128:(tt + 1) * 128, :]))(),
                 writes=[b_xs[sl]], dma=s_xs[sl])
            B.op(ACT, (lambda sl=sl: lambda e: e.activation(out=junk, in_=xs[sl], func=AF.Square, accum_out=st[sl][:, 0:1]))(),
                 reads=[b_xs[sl]], writes=[b_junk, b_st[sl]])
            B.op(ACT, (lambda sl=sl: lambda e: e.activation(out=st[sl][:, 1:2], in_=st[sl][:, 0:1], func=AF.Sqrt,
                                                             scale=1.0 / D, bias=EPS))(), reads=[b_st[sl]], writes=[b_st[sl]])
            B.op(DVE, (lambda sl=sl: lambda e: e.reciprocal(out=st[sl][:, 2:3], in_=st[sl][:, 1:2]))(),
                 reads=[b_st[sl]], writes=[b_st[sl]])
            B.op(DVE, (lambda sl=sl: lambda e: e.scalar_tensor_tensor(out=htmp[sl], in0=xs[sl], scalar=st[sl][:, 2:3], in1=a1,
                                                                       op0=ALU.mult, op1=ALU.mult))(),
                 reads=[b_xs[sl], b_st[sl]], writes=[b_htmp[sl]])
            B.op(POOL, lambda e: e.tensor_tensor(out=hb[sl], in0=htmp[sl], in1=sh1, op=ALU.add), reads=[b_htmp[sl]], writes=[b_hb[sl]])
            B.wait_only(PE, reads=[b_hb[sl]], writes=[bankb[0]])
            for kc in range(8):
                fn = (lambda kc=kc: lambda e: e.transpose(tpA[:, kc * 128:(kc + 1) * 128], hb[sl][:, kc * 128:(kc + 1) * 128], ident_bf))()
                if kc < 7:
                    B.raw(PE, fn)
                else:
                    B.op(PE, fn, reads=[b_hb[sl]], writes=[bankb[0]])

        def pA_F1b(tt):
            sl = tt % 2
            B.op(ACT, lambda e: e.activation(out=hT[sl], in_=tpA, func=AF.Identity), reads=[bankb[0]], writes=[b_hT[sl]])

        def pA_F2(tt):
            sl = tt % 2
            B.wait_only(PE, reads=[b_hT[sl]], writes=[bankb[1], bankb[2], bankb[3]])
            for kc in range(8):
                for ni, (n0, nn) in enumerate([(0, 512), (512, 512), (1024, 256)]):
                    fn = (lambda kc=kc, ni=ni, n0=n0, nn=nn: lambda e: e.matmul(
                        pj[ni][:, 0:nn], lhsT=hT[sl][:, kc * 128:(kc + 1) * 128],
                        rhs=w_in_bf[:, kc * 1280 + n0:kc * 1280 + n0 + nn], start=(kc == 0), stop=(kc == 7)))()
                    if kc == 7 and ni == 2:
                        B.op(PE, fn, reads=[b_hT[sl]], writes=[bankb[1], bankb[2], bankb[3]])
                    else:
                        B.raw(PE, fn)

        def pA_B1(tt):
            sl = tt % 2
            B.op(ACT, (lambda tt=tt: lambda e: e.activation(
                out=VA.rearrange("p (k a b) -> p k a b", a=3, b=64)[:, tt, 0:3:2, :], in_=v3(pj[1][:, 128:256], 2), func=AF.Identity))(),
                reads=[bankb[2]], writes=[b_VA])
            us = tt % 4
            B.op(ACT, (lambda us=us: lambda e: e.activation(out=upool[us][:, 0:256], in_=pj[1][:, 256:512], func=AF.Identity))(),
                 reads=[bankb[2]], writes=[b_up[us]])
            B.op(ACT, (lambda us=us: lambda e: e.activation(out=upool[us][:, 256:512], in_=pj[2][:, 0:256], func=AF.Identity))(),
                 reads=[bankb[3]], writes=[b_up[us]])
            B.op(ACT, lambda e: e.activation(out=sq[:, 0:512], in_=pj[0][:, :], func=AF.Square), reads=[bankb[1]], writes=[b_sq])
            B.op(ACT, lambda e: e.activation(out=sq[:, 512:640], in_=pj[1][:, 0:128], func=AF.Square), reads=[bankb[2]], writes=[b_sq])
            B.op(DVE, lambda e: e.tensor_reduce(out=s10[:, 0:10], in_=v3(sq, 10), axis=AX.X, op=ALU.add),
                 reads=[b_sq], writes=[b_s10])
            B.op(ACT, lambda e: e.activation(out=s10[:, 10:20], in_=s10[:, 0:10], func=AF.Sqrt, scale=1.0 / 64, bias=EPS),
                 reads=[b_s10], writes=[b_s10])
            B.op(DVE, lambda e: e.reciprocal(out=s10[:, 20:30], in_=s10[:, 10:20]), reads=[b_s10], writes=[b_s10])
            B.op(DVE, lambda e: e.tensor_tensor(out=v3(qn[:, 0:512], 8), in0=v3(pj[0][:, :], 8),
                                                in1=s10[:, 20:28].unsqueeze(2).to_broadcast([128, 8, 64]), op=ALU.mult),
                 reads=[bankb[1], b_s10], writes=[b_qn])
            B.op(DVE, lambda e: e.tensor_tensor(out=v3(qn[:, 512:640], 2), in0=v3(pj[1][:, 0:128], 2),
                                                in1=s10[:, 28:30].unsqueeze(2).to_broadcast([128, 2, 64]), op=ALU.mult),
                 reads=[bankb[2], b_s10], writes=[b_qn])

        def pA_B2(tt):
            sl = tt % 2
            B.op(POOL, lambda e: e.tensor_tensor(out=qg, in0=qn, in1=gv, op=ALU.mult), reads=[b_qn], writes=[b_qg])
            B.op(POOL, (lambda tt=tt: lambda e: e.tensor_tensor(
                out=v3(Aa, 10), in0=v3(qg, 10),
                in1=ropeC[:, tt * 64:(tt + 1) * 64].unsqueeze(1).to_broadcast([128, 10, 64]), op=ALU.mult))(),
                reads=[b_qg, b_rope], writes=[b_Aa])

            def v4(ap):
                return ap.rearrange("p (a r h s) -> p a r h s", a=10, r=2, h=2, s=16)

            def s4(tt, h):
                return ropeS[:, tt * 64:(tt + 1) * 64].rearrange("p (r h s) -> p r h s", r=2, h=2, s=16)[:, :, h, :] \
                    .unsqueeze(1).to_broadcast([128, 10, 2, 16])
            B.op(DVE, (lambda tt=tt: lambda e: e.tensor_tensor(out=v4(Bt)[:, :, :, 0, :], in0=v4(qg)[:, :, :, 1, :],
                                                               in1=s4(tt, 0), op=ALU.mult))(),
                 reads=[b_qg, b_rope], writes=[b_Bt])
            B.op(DVE, (lambda tt=tt: lambda e: e.tensor_tensor(out=v4(Bt)[:, :, :, 1, :], in0=v4(qg)[:, :, :, 0, :],
                                                               in1=s4(tt, 1), op=ALU.mult))(),
                 reads=[b_qg, b_rope], writes=[b_Bt])
            B.op(DVE, lambda e: e.tensor_tensor(
                out=qrope.rearrange("p (c e d) -> p e c d", c=4, e=2, d=64),
                in0=Aa[:, 0:512].rearrange("p (e c d) -> p e c d", e=2, c=4, d=64),
                in1=Bt[:, 0:512].rearrange("p (e c d) -> p e c d", e=2, c=4, d=64), op=ALU.add),
                reads=[b_Aa, b_Bt], writes=[b_qrope])
            B.op(POOL, lambda e: e.tensor_tensor(out=krope, in0=Aa[:, 512:640], in1=Bt[:, 512:640], op=ALU.add),
                 reads=[b_Aa, b_Bt], writes=[b_krope])
            B.wait_only(PE, reads=[b_qrope, b_krope], writes=[bankb[4]])
            for c in range(4):
                B.raw(PE, (lambda c=c: lambda e: e.transpose(tq[:, c * 128:(c + 1) * 128], qrope[:, c * 128:(c + 1) * 128], ident_bf))())
            B.op(PE, lambda e: e.transpose(tq[:, 512:640], krope, ident_bf), reads=[b_qrope, b_krope], writes=[bankb[4]])
            B.op(ACT, (lambda tt=tt: lambda e: e.activation(out=QT3[:, :, tt * 128:(tt + 1) * 128], in_=v3(tq[:, 0:512], 4), func=AF.Identity))(),
                 reads=[bankb[4]], writes=[b_QT])
            B.op(ACT, (lambda tt=tt: lambda e: e.activation(out=KT[:, tt * 128:(tt + 1) * 128], in_=tq[:, 512:640], func=AF.Identity))(),
                 reads=[bankb[4]], writes=[b_KT])

        pA_F1a(0)
        pA_F1b(0)
        pA_F2(0)
        for tt in range(NTT):
            if tt + 1 < NTT:
                pA_F1a(tt + 1)
            pA_B1(tt)
            if tt + 1 < NTT:
                pA_F1b(tt + 1)
                pA_F2(tt + 1)
            pA_B2(tt)
            if tt >= 1:
                pool_tile(tt - 1)
        pool_tile(NTT - 1)
        if dbg:
            s_d1 = B.new_dma_sem("s_d1")
            B.barrier()
            B.op(SP, lambda e: e.dma_start(out=d_qt, in_=QT), dma=s_d1)
            B.op(SP, lambda e: e.dma_start(out=d_kt, in_=KT), dma=B.new_dma_sem("s_d2"))
            B.op(SP, lambda e: e.dma_start(out=d_v, in_=VA), dma=B.new_dma_sem("s_d3"))
            B.op(SP, lambda e: e.dma_start(out=d_pt, in_=PT), dma=B.new_dma_sem("s_d4"))
        B.barrier()

        if stop_after == 'A':
            B.emit()
            return nc
        A.off = ab_end
        wout_sb = A.alloc(8192, BF16)
        attT = [A.alloc(2048, BF16) for _ in range(2)]
        NPT = 4
        pt = [A.alloc(1024, BF16) for _ in range(NPT)]
        xin = [A.alloc(1024, F32) for _ in range(2)]
        x1o = [A.alloc(1024, F32) for _ in range(2)]
        rsum = [A.alloc(512, F32) for _ in range(2)]
        bcs = [A.alloc(512, F32) for _ in range(2)]
        b_wout = Buf("wout")
        b_att = [Buf("att%d" % i) for i in range(2)]
        b_pt = [Buf("pt%d" % i) for i in range(NPT)]
        b_xin = [Buf("xin%d" % i) for i in range(2)]
        b_x1o = [Buf("x1o%d" % i) for i in range(2)]
        b_rsum = [Buf("rsum%d" % i) for i in range(2)]
        b_bcs = [Buf("bcs%d" % i) for i in range(2)]
        s_wout = B.new_dma_sem("s_wout")
        s_xin = [B.new_dma_sem("s_xin%d" % i) for i in range(2)]
        s_x1o = [B.new_dma_sem("s_x1o%d" % i) for i in range(2)]
        B.op(SP, lambda e: e.dma_start(out=wout_sb, in_=w_out_bf), writes=[b_wout], dma=s_wout)
        b_S = [Buf("Spair0"), Buf("Spair1")]
        sO = [4, 5]
        sB = 6
        sP = 6
        NDUMMY = 2
        NKP = NTT // 2
        items = [(qs, c, e_, kp) for qs in range(8) for c in range(4) for e_ in range(2) for kp in range(NKP)]
        NI = len(items)
        pending = []
        xcnt = [0]

        def emit_qk(i):
            qs, c, e_, kp = items[i]
            sp = i % 2
            B.wait_only(PE, writes=[b_S[sp]])
            for u in range(2):
                kt = 2 * kp + u
                fn = (lambda kt=kt, u=u: lambda e: e.matmul(
                    banks[2 * sp + u][:, :], lhsT=KT[e_ * 64:(e_ + 1) * 64, kt * 128:(kt + 1) * 128],
                    rhs=QT3[e_ * 64:(e_ + 1) * 64, c, qs * 512:(qs + 1) * 512], start=True, stop=True))()
                if u == 0:
                    B.raw(PE, fn)
                else:
                    B.op(PE, fn, writes=[b_S[sp]])

        def emit_exp(i):
            sp = i % 2
            B.op(ACT, lambda e: e.activation(out=pt[i % NPT], in_=ps_all[:, sp * 1024:(sp + 1) * 1024], func=AF.Exp,
                                             bias=negM[:, 0:1], scale=1.0),
                 reads=[b_S[sp]], writes=[b_pt[i % NPT]])

        def emit_pv(i):
            qs, c, e_, kp = items[i]
            hidx = i // NKP
            bk = sO[hidx % 2]
            for u in range(2):
                kt = 2 * kp + u
                if e_ == 0:
                    lhsT = VA3[:, kt, 0:65]
                    o_ = banks[bk][0:65, :]
                else:
                    lhsT = VA3[:, kt, 64:192]
                    o_ = banks[bk][:, :]
                fn = (lambda lhsT=lhsT, o_=o_, kt=kt, u=u: lambda e: e.matmul(
                    o_, lhsT=lhsT, rhs=pt[i % NPT][:, u * 512:(u + 1) * 512], start=(kt == 0), stop=(kt == NTT - 1)))()
                if u == 0:
                    wr = [bankb[bk]] if kt == 0 else []
                    B.op(PE, fn, reads=[b_pt[i % NPT]], writes=wr, regreads=False)
                else:
                    wr = [bankb[bk]] if kt == NTT - 1 else []
                    if wr:
                        B.op(PE, fn, reads=[b_pt[i % NPT]], writes=wr, regreads=False)
                    else:
                        B.raw(PE, fn)

        def outproj(qs):
            ab = qs % 2
            for j in range(4):
                xsl = xcnt[0] % 2
                xcnt[0] += 1
                r0 = qs * 512 + j * 128
                B.op(SP, lambda e, r0=r0, xsl=xsl: e.dma_start(out=xin[xsl], in_=x[r0:r0 + 128, :]),
                     writes=[b_xin[xsl]], dma=s_xin[xsl])
                for half in range(2):
                    bk = sP
                    B.wait_only(PE, reads=[b_att[ab], b_wout], writes=[bankb[bk]])
                    for cc in range(8):
                        if cc < 4:
                            lhsT = attT[ab][:, cc * 512 + j * 128:cc * 512 + (j + 1) * 128]
                        else:
                            lhsT = PT3[:, cc - 4, r0:r0 + 128]
                        fn = (lambda lhsT=lhsT, cc=cc, half=half, bk=bk: lambda e: e.matmul(
                            banks[bk][:, :], lhsT=lhsT, rhs=wout_sb[:, cc * 1024 + half * 512:cc * 1024 + (half + 1) * 512],
                            start=(cc == 0), stop=(cc == 7)))()
                        if cc < 7:
                            B.raw(PE, fn)
                        else:
                            B.op(PE, fn, reads=[b_att[ab], b_wout], writes=[bankb[bk]])
                    B.op(DVE, lambda e, half=half, bk=bk, xsl=xsl: e.tensor_tensor(
                        out=x1o[xsl][:, half * 512:(half + 1) * 512], in0=banks[bk][:, :],
                        in1=xin[xsl][:, half * 512:(half + 1) * 512], op=ALU.add),
                        reads=[bankb[bk], b_xin[xsl]], writes=[b_x1o[xsl]])
                B.op(POOL, lambda e, r0=r0, xsl=xsl: e.dma_start(out=x1s[1 + r0:1 + r0 + 128, :], in_=x1o[xsl]),
                     reads=[b_x1o[xsl]], dma=s_x1o[xsl])

        def head_post(i):
            qs, c, e_, kp = items[i]
            hidx = i // NKP
            bk = sO[hidx % 2]
            hs = hidx % 2
            srow = 64 if e_ == 0 else 0
            r0, r1 = e_ * 64, (e_ + 1) * 64
            B.op(DVE, lambda e: e.reciprocal(out=rsum[hs][srow:srow + 1, :], in_=banks[bk][srow:srow + 1, :]),
                 reads=[bankb[bk]], writes=[b_rsum[hs]])

            def pe_bc():
                B.op(PE, lambda e: e.matmul(banks[sB][:, :], lhsT=ones_f[srow:srow + 1, 0:128], rhs=rsum[hs][srow:srow + 1, :],
                                            start=True, stop=True), reads=[b_rsum[hs]], writes=[bankb[sB]])
                B.op(DVE, lambda e: e.tensor_copy(out=bcs[hs][r0:r1, :], in_=banks[sB][r0:r1, :]),
                     reads=[bankb[sB]], writes=[b_bcs[hs]])
                B.op(DVE, lambda e: e.tensor_tensor(out=attT[qs % 2][r0:r1, c * 512:(c + 1) * 512], in0=banks[bk][r0:r1, :],
                                                    in1=bcs[hs][r0:r1, :], op=ALU.mult),
                     reads=[bankb[bk], b_bcs[hs]], writes=[b_att[qs % 2]])
                if c == 3 and e_ == 1:
                    pending.append((i + 4, lambda: outproj(qs)))
            pending.append((i + 3, pe_bc))

        def emit_dummy():
            for _ in range(NDUMMY):
                B.raw(PE, lambda e: e.matmul(banks[7][:, :], lhsT=KT[:, 0:128], rhs=QT3[:, 0, 0:512], start=True, stop=True))

        emit_qk(0)
        for i in range(NI):
            emit_exp(i)
            if i + 1 < NI:
                emit_qk(i + 1)
            emit_dummy()
            emit_pv(i)
            if items[i][3] == NKP - 1:
                head_post(i)
            while pending and pending[0][0] <= i:
                pending.pop(0)[1]()
        while pending:
            pending.pop(0)[1]()
        if dbg:
            B.barrier()
            B.op(SP, lambda e: e.dma_start(out=d_x1, in_=x1s), dma=B.new_dma_sem("s_d5"))
        B.barrier()

        if stop_after == 'B':
            B.emit()
            return nc
        A.off = base
        wdn = A.alloc(NCH * 1024, BF16)
        actT = A.alloc(NCH * 512, BF16)
        h2T = [A.alloc(8 * 512, BF16) for _ in range(2)]
        x1u = [A.alloc(1024, F32) for _ in range(2)]
        st2 = [A.alloc(16, F32) for _ in range(2)]
        junk2 = A.alloc(1024, BF16)
        htmp2 = A.alloc(1024, F32)
        h2b = [A.alloc(1024, BF16) for _ in range(2)]
        NW = 4
        wup = [A.alloc(2048, BF16) for _ in range(NW)]
        Tg = [A.alloc(512, F32) for _ in range(2)]
        Tv = [A.alloc(512, F32) for _ in range(2)]
        Sg = [A.alloc(512, F32) for _ in range(2)]
        xr = [A.alloc(1024, F32) for _ in range(2)]
        outt = [A.alloc(1024, F32) for _ in range(2)]
        actT3 = actT.rearrange("p (c t) -> p c t", c=NCH)
        b_wdn = Buf("wdn")
        b_actT = [Buf("actT%d" % i) for i in range(NCH)]
        b_h2T = [Buf("h2T%d" % i) for i in range(2)]
        b_x1u = [Buf("x1u%d" % i) for i in range(2)]
        b_st2 = [Buf("st2%d" % i) for i in range(2)]
        b_junk2, b_htmp2 = Buf("junk2"), Buf("htmp2")
        b_h2b = [Buf("h2b%d" % i) for i in range(2)]
        b_wup = [Buf("wup%d" % i) for i in range(NW)]
        b_Tg = [Buf("Tg%d" % i) for i in range(2)]
        b_Tv = [Buf("Tv%d" % i) for i in range(2)]
        b_Sg = [Buf("Sg%d" % i) for i in range(2)]
        b_xr = [Buf("xr%d" % i) for i in range(2)]
        b_outt = [Buf("outt%d" % i) for i in range(2)]
        s_wdn = B.new_dma_sem("s_wdn")
        s_x1u = [B.new_dma_sem("s_x1u%d" % i) for i in range(2)]
        s_wup = [B.new_dma_sem("s_wup%d" % i) for i in range(NW)]
        s_xr = [B.new_dma_sem("s_xr%d" % i) for i in range(2)]
        s_outt = [B.new_dma_sem("s_outt%d" % i) for i in range(2)]
        B.op(SP, lambda e: e.dma_start(out=wdn, in_=w_down_bf), writes=[b_wdn], dma=s_wdn)
        tp2 = banks[0][:, :].bitcast(BF16)
        Gb = [1, 2]
        Vb = [3, 4]
        Yb = [5, 6]
        stages = [(510 * i, 510) for i in range(8)] + [(4080, 16)]
        NS = len(stages)
        pcnt = [0]
        pe_defer = []

        def prep_tile(si, j):
            o, n = stages[si]
            N = n + 2
            m = min(128, N - j * 128)
            hs_ = si % 2
            k = pcnt[0] % 2
            pcnt[0] += 1
            r0 = o + j * 128
            h2T3 = h2T[hs_].rearrange("p (k t) -> p k t", k=8)
            B.op(SP, lambda e: e.dma_start(out=x1u[k][0:m, :], in_=x1s[r0:r0 + m, :]), writes=[b_x1u[k]], dma=s_x1u[k])
            B.op(ACT, lambda e: e.activation(out=junk2[0:m, :], in_=x1u[k][0:m, :], func=AF.Square, accum_out=st2[k][0:m, 0:1]),
                 reads=[b_x1u[k]], writes=[b_junk2, b_st2[k]])
            B.op(ACT, lambda e: e.activation(out=st2[k][0:m, 1:2], in_=st2[k][0:m, 0:1], func=AF.Sqrt, scale=1.0 / D, bias=EPS),
                 reads=[b_st2[k]], writes=[b_st2[k]])
            B.op(DVE, lambda e: e.reciprocal(out=st2[k][0:m, 2:3], in_=st2[k][0:m, 1:2]), reads=[b_st2[k]], writes=[b_st2[k]])
            B.op(DVE, lambda e: e.scalar_tensor_tensor(out=htmp2[0:m, :], in0=x1u[k][0:m, :], scalar=st2[k][0:m, 2:3],
                                                       in1=a2[0:m, :], op0=ALU.mult, op1=ALU.mult),
                 reads=[b_x1u[k], b_st2[k]], writes=[b_htmp2])
            B.op(POOL, lambda e: e.tensor_tensor(out=h2b[k][0:m, :], in0=htmp2[0:m, :], in1=sh2[0:m, :], op=ALU.add),
                 reads=[b_htmp2], writes=[b_h2b[k]])

            def pe_part():
                B.wait_only(PE, reads=[b_h2b[k]], writes=[bankb[0]])
                for kc in range(8):
                    fn = (lambda kc=kc: lambda e: e.transpose(tp2[:, kc * 128:kc * 128 + m], h2b[k][0:m, kc * 128:(kc + 1) * 128],
                                                              ident_bf[0:m, 0:m]))()
                    if kc < 7:
                        B.raw(PE, fn)
                    else:
                        B.op(PE, fn, reads=[b_h2b[k]], writes=[bankb[0]])
                B.op(ACT, lambda e: e.activation(out=h2T3[:, :, j * 128:j * 128 + m],
                                                 in_=tp2.rearrange("p (k t) -> p k t", k=8)[:, :, 0:m], func=AF.Identity),
                     reads=[bankb[0]], writes=[b_h2T[hs_]])
                if si == 0 and j == 0:
                    B.op(POOL, lambda e: e.memset(h2T3[:, :, 0:1], 0.0), writes=[b_h2T[hs_]])
                if si == NS - 1 and j * 128 + m == N:
                    B.op(POOL, lambda e: e.memset(h2T3[:, :, N - 1:N], 0.0), writes=[b_h2T[hs_]])
            return pe_part

        def ntiles(si):
            return (stages[si][1] + 2 + 127) // 128

        wcnt = [0]
        wq = []

        def issue_wup(c):
            k = wcnt[0] % NW
            wcnt[0] += 1
            B.op(SP, lambda e: e.dma_start(out=wup[k], in_=w_up_bf[:, c * 2048:(c + 1) * 2048]), writes=[b_wup[k]], dma=s_wup[k])
            wq.append(k)

        for j in range(ntiles(0)):
            prep_tile(0, j)()
        allc = [(si, c) for si in range(NS) for c in range(NCH)]
        for q_ in range(min(NW - 1, len(allc))):
            issue_wup(allc[q_][1])
        ccnt = [0]
        ycnt = [0]
        ocnt = [0]

        def ffn_stage(si):
            o, n = stages[si]
            N = n + 2
            hs_ = si % 2
            h2T3 = h2T[hs_].rearrange("p (k t) -> p k t", k=8)
            nprep = ntiles(si + 1) if si + 1 < NS else 0
            prep_at = {3 + 4 * jj: jj for jj in range(nprep)}
            for c in range(NCH):
                gi = si * NCH + c
                if gi + NW - 1 < len(allc):
                    issue_wup(allc[gi + NW - 1][1])
                k = wq.pop(0)
                b2 = ccnt[0] % 2
                ccnt[0] += 1
                for which, bkl, off in ((0, Gb, 0), (1, Vb, 128)):
                    bk = bkl[b2]
                    B.wait_only(PE, reads=[b_wup[k], b_h2T[hs_]], writes=[bankb[bk]])
                    for kc in range(8):
                        fn = (lambda kc=kc, bk=bk, off=off, k=k: lambda e: e.matmul(
                            banks[bk][:, 0:N], lhsT=wup[k][:, kc * 256 + off:kc * 256 + off + 128],
                            rhs=h2T3[:, kc, 0:N], start=(kc == 0), stop=(kc == 7)))()
                        if kc < 7:
                            B.raw(PE, fn)
                        else:
                            B.op(PE, fn, reads=[b_wup[k], b_h2T[hs_]], writes=[bankb[bk]])
                for which, bkl, Tl, bTl, cc in ((0, Gb, Tg, b_Tg, c), (1, Vb, Tv, b_Tv, NCH + c)):
                    bk = bkl[b2]
                    Tt = Tl[b2]
                    bT = bTl[b2]
                    B.op(ACT, lambda e, bk=bk, Tt=Tt, cc=cc: e.activation(
                        out=Tt[:, 0:n], in_=banks[bk][:, 0:n], func=AF.Identity,
                        scale=convp[:, cc * 4:cc * 4 + 1], bias=convp[:, cc * 4 + 3:cc * 4 + 4]),
                        reads=[bankb[bk]], writes=[bT])
                    B.op(DVE, lambda e, bk=bk, Tt=Tt, cc=cc: e.scalar_tensor_tensor(
                        out=Tt[:, 0:n], in0=banks[bk][:, 1:n + 1], scalar=convp[:, cc * 4 + 1:cc * 4 + 2], in1=Tt[:, 0:n],
                        op0=ALU.mult, op1=ALU.add), reads=[bankb[bk], bT], writes=[bT])
                    B.op(DVE, lambda e, bk=bk, Tt=Tt, cc=cc: e.scalar_tensor_tensor(
                        out=Tt[:, 0:n], in0=banks[bk][:, 2:n + 2], scalar=convp[:, cc * 4 + 2:cc * 4 + 3], in1=Tt[:, 0:n],
                        op0=ALU.mult, op1=ALU.add), reads=[bankb[bk], bT], writes=[bT])
                B.op(ACT, lambda e, b2=b2: e.activation(out=Sg[b2][:, 0:n], in_=Tg[b2][:, 0:n], func=AF.Silu),
                     reads=[b_Tg[b2]], writes=[b_Sg[b2]])
                B.op(DVE, lambda e, b2=b2, c=c: e.tensor_tensor(out=actT3[:, c, 0:n], in0=Sg[b2][:, 0:n], in1=Tv[b2][:, 0:n],
                                                               op=ALU.mult),
                     reads=[b_Sg[b2], b_Tv[b2]], writes=[b_actT[c]])
                if c in prep_at:
                    pe_defer.append((c + 2, prep_tile(si + 1, prep_at[c])))
                while pe_defer and pe_defer[0][0] <= c:
                    pe_defer.pop(0)[1]()
            while pe_defer:
                pe_defer.pop(0)[1]()
            for j in range((n + 127) // 128):
                m = min(128, n - j * 128)
                xk = ocnt[0] % 2
                ocnt[0] += 1
                r0 = o + j * 128
                B.op(SP, lambda e, xk=xk, r0=r0, m=m: e.dma_start(out=xr[xk][0:m, :], in_=x1s[1 + r0:1 + r0 + m, :]),
                     writes=[b_xr[xk]], dma=s_xr[xk])
                for half in range(2):
                    yk = Yb[ycnt[0] % 2]
                    ycnt[0] += 1
                    B.wait_only(PE, reads=b_actT + [b_wdn], writes=[bankb[yk]])
                    for c in range(NCH):
                        fn = (lambda c=c, yk=yk, half=half, j=j, m=m: lambda e: e.matmul(
                            banks[yk][0:m, :], lhsT=actT3[:, c, j * 128:j * 128 + m],
                            rhs=wdn[:, c * 1024 + half * 512:c * 1024 + (half + 1) * 512],
                            start=(c == 0), stop=(c == NCH - 1)))()
                        if c < NCH - 1:
                            B.raw(PE, fn)
                        else:
                            B.op(PE, fn, reads=b_actT + [b_wdn], writes=[bankb[yk]])
                    B.op(DVE, lambda e, yk=yk, xk=xk, half=half, m=m: e.tensor_tensor(
                        out=outt[xk][0:m, half * 512:(half + 1) * 512], in0=banks[yk][0:m, :],
                        in1=xr[xk][0:m, half * 512:(half + 1) * 512], op=ALU.add),
                        reads=[bankb[yk], b_xr[xk]], writes=[b_outt[xk]])
                B.op(POOL, lambda e, xk=xk, r0=r0, m=m: e.dma_start(out=out[r0:r0 + m, :], in_=outt[xk][0:m, :]),
                     reads=[b_outt[xk]], dma=s_outt[xk])
        for si in range(NS):
            ffn_stage(si)
        B.barrier()
        B.emit()
    return nc


def _rope_tables():
    t = np.arange(T)
    row = (t // 64).astype(np.float32)
    col = (t % 64).astype(np.float32)
    inv = (1.0 / (np.float32(10000.0) ** (np.arange(0, 32, 2, dtype=np.float32) / np.float32(32)))).astype(np.float32)
    ar = (row[:, None] * inv[None, :]).astype(np.float32)
    ac = (col[:, None] * inv[None, :]).astype(np.float32)
    cr, sr, cc, sc = np.cos(ar), np.sin(ar), np.cos(ac), np.sin(ac)
    C = np.concatenate([cr, cr, cc, cc], axis=1).astype(np.float32)
    S = np.concatenate([-sr, sr, -sc, sc], axis=1).astype(np.float32)
    def lay(a):
        return np.ascontiguousarray(a.reshape(NTT, 128, 64).transpose(1, 0, 2).reshape(128, NTT * 64))
    return lay(C), lay(S)


def _band_tables():
    band = np.zeros((128, 4, 5, 128), np.float32)
    for g, w in enumerate((2, 4, 8, 16)):
        def blk(t_out0, t_src0):
            M = np.zeros((128, 128), np.float32)
            for i in range(128):
                tt = t_out0 + i
                lo = max(tt - w // 2, 0)
                hi = min(tt + w // 2 - 1, T - 1)
                cnt = hi - lo + 1
                for ts in range(lo, hi + 1):
                    j = ts - t_src0
                    if 0 <= j < 128:
                        M[j, i] += 1.0 / cnt
                j = tt - t_src0
                if 0 <= j < 128:
                    M[j, i] -= 1.0
            return M
        band[:, g, 0] = blk(1280, 1152)
        band[:, g, 1] = blk(1280, 1280)
        band[:, g, 2] = blk(1280, 1408)
        band[:, g, 3] = blk(0, 0)
        band[:, g, 4] = blk(T - 128, T - 128)
    return np.ascontiguousarray(band.reshape(128, 2560))


_CACHE = {}


def _shared_inputs(w_ada, b_ada, norm1_g, w_in, q_norm_g, k_norm_g, w_pool, pool_scale, w_out, norm2_g,
                   w_up, conv_w, conv_b, w_down):
    f = np.float32
    d = {}
    d["w_ada_l"] = np.ascontiguousarray(
        np.asarray(w_ada[0], f).reshape(8, 128, 12, 512).transpose(1, 2, 0, 3).reshape(128, 12 * 4096))
    d["b_ada"] = np.ascontiguousarray(np.asarray(b_ada[0], f).reshape(1, 6144))
    d["n1g"] = np.ascontiguousarray(np.asarray(norm1_g[0], f).reshape(1, D))
    d["n2g"] = np.ascontiguousarray(np.asarray(norm2_g[0], f).reshape(1, D))
    d["w_in_l"] = np.ascontiguousarray(np.asarray(w_in[0], f).reshape(8, 128, 1280).transpose(1, 0, 2).reshape(128, 8 * 1280))
    d["gvec"] = np.ascontiguousarray(
        np.concatenate([np.tile(np.asarray(q_norm_g[0], f), 8), np.tile(np.asarray(k_norm_g[0], f), 2)]).reshape(1, 640))
    d["w_pool_l"] = np.ascontiguousarray(np.asarray(w_pool[0], f).transpose(1, 0, 2).reshape(128, 512))
    d["pscale_l"] = np.ascontiguousarray(np.asarray(pool_scale[0], f).reshape(4, 128).T)
    wo = np.asarray(w_out[0], f)
    chunks = []
    for c in range(4):
        chunks.append(np.concatenate([wo[c * 64:(c + 1) * 64], wo[(4 + c) * 64:(5 + c) * 64]], axis=0))
    for g in range(4):
        chunks.append(wo[512 + g * 128:512 + (g + 1) * 128])
    d["w_out_l"] = np.ascontiguousarray(np.stack(chunks, axis=1).reshape(128, 8 * 1024))
    wu = np.asarray(w_up[0], f).reshape(8, 128, 2, NCH, 128)
    d["w_up_l"] = np.ascontiguousarray(wu.transpose(1, 3, 0, 2, 4).reshape(128, NCH * 2048))
    cw = np.asarray(conv_w[0], f)
    cb = np.asarray(conv_b[0], f)
    cp = np.stack([cw[0], cw[1], cw[2], cb], axis=1).reshape(44, 128, 4).transpose(1, 0, 2)
    d["convp"] = np.ascontiguousarray(cp.reshape(128, 176))
    d["w_down_l"] = np.ascontiguousarray(np.asarray(w_down[0], f).reshape(NCH, 128, 1024).transpose(1, 0, 2).reshape(128, NCH * 1024))
    if "rope" not in _CACHE:
        _CACHE["rope"] = _rope_tables()
        _CACHE["band"] = _band_tables()
    d["ropeC"], d["ropeS"] = _CACHE["rope"]
    d["band"] = _CACHE["band"]
    d["ident"] = np.eye(128, dtype=f)
    return d


def kernel(x, c, w_ada, b_ada, norm1_g, w_in, q_norm_g, k_norm_g, w_pool, pool_scale,
           w_out, norm2_g, w_up, conv_w, conv_b, w_down):
    x = np.asarray(x, np.float32)
    c = np.asarray(c, np.float32)
    shared = _shared_inputs(w_ada, b_ada, norm1_g, w_in, q_norm_g, k_norm_g, w_pool, pool_scale, w_out, norm2_g,
                            w_up, conv_w, conv_b, w_down)
    if "nc" not in _CACHE:
        _CACHE["nc"] = build_program()
    nc = _CACHE["nc"]
    in_maps = []
    for b in range(8):
        m = dict(shared)
        m["x"] = np.ascontiguousarray(x[b])
        m["c_l"] = np.ascontiguousarray(c[b].reshape(8, 128).T)
        in_maps.append(m)
    res = run_bass_kernel_spmd(nc, in_maps, core_ids=list(range(8)))
    return np.stack([np.asarray(r["out"], np.float32) for r in res.results], axis=0)
```
